# Optimizing a Trainium2 kernel written in Bass

```python
import math
import jax, jax.numpy as jnp
from jax import lax
import numpy as np

D_MODEL = 1024
BATCH = 8
SEQ = 2048
DEPTH = 1
DEC_BATCH = 128
DEC_SEQ = 1
PAST_LEN = 16384
PAGE_SIZE = 128

W_A = D_MODEL
N_A_GROUPS = 8
W_B = D_MODEL
N_B_HEADS = 8
HEAD_B = W_B // N_B_HEADS
W_MIX = W_A + W_B
W_IN = 3 * W_A + 2 * W_B
CONV_A_WIDTH = 31
CONV_B_WIDTH = 4
RG_C = 8.0
PLE_DIM = 256
EPS = 1e-6

kernel_name = "hymba_conformer_rglru_step"


def _rmsnorm(x, g):
    xf = x.astype(jnp.float32)
    y = xf * lax.rsqrt(jnp.mean(xf * xf, axis=-1, keepdims=True) + EPS)
    return (y * g.astype(jnp.float32)).astype(x.dtype)


def _layernorm(x, g, b):
    xf = x.astype(jnp.float32)
    mu = jnp.mean(xf, axis=-1, keepdims=True)
    xc = xf - mu
    var = jnp.mean(xc * xc, axis=-1, keepdims=True)
    y = xc * lax.rsqrt(var + EPS) * g.astype(jnp.float32) + b.astype(jnp.float32)
    return y.astype(x.dtype)


def _causal_dwconv(x, buf, w, b):
    k = w.shape[0]
    xp = jnp.concatenate([buf.astype(x.dtype), x], axis=1)
    y = lax.conv_general_dilated(
        xp, w[:, None, :].astype(x.dtype), window_strides=(1,), padding='VALID',
        dimension_numbers=('NWC', 'WIO', 'NWC'), feature_group_count=x.shape[-1])
    new_buf = xp[:, xp.shape[1] - (k - 1):]
    return y + b.astype(x.dtype), new_buf


def _rglru(x, h0, w_r, b_r, w_i, b_i, lam):
    bsz, t, w = x.shape
    xh = x.reshape(bsz, t, N_B_HEADS, HEAD_B)
    r = jax.nn.sigmoid(jnp.einsum('bthi,hij->bthj', xh, w_r.astype(x.dtype)).reshape(bsz, t, w) + b_r.astype(x.dtype))
    i = jax.nn.sigmoid(jnp.einsum('bthi,hij->bthj', xh, w_i.astype(x.dtype)).reshape(bsz, t, w) + b_i.astype(x.dtype))
    log_a = -RG_C * r.astype(jnp.float32) * jax.nn.softplus(-lam.astype(jnp.float32))
    a = jnp.exp(log_a)
    bx = jnp.sqrt(-jnp.expm1(2.0 * log_a)) * (i * x).astype(jnp.float32)

    def step(hc, ab):
        a_t, b_t = ab
        hc = a_t * hc + b_t
        return hc, hc

    h_last, hs = lax.scan(step, h0.astype(jnp.float32), (jnp.swapaxes(a, 0, 1), jnp.swapaxes(bx, 0, 1)))
    return jnp.swapaxes(hs, 0, 1).astype(x.dtype), h_last


def _layer(h, p, buf_a, buf_b, h0, g_norm, w_in, w_dw_a, b_dw_a, ln_g, ln_b,
           w_conv_b, b_conv_b, w_r, b_r, w_i, b_i, lam, w_out, w_pe, w_pg):
    u = _rmsnorm(h, g_norm)
    z = u @ w_in.astype(u.dtype)
    a_val = z[..., :W_A]
    a_glu = z[..., W_A:2 * W_A]
    a_gate = z[..., 2 * W_A:3 * W_A]
    b_x = z[..., 3 * W_A:3 * W_A + W_B]
    b_gate = z[..., 3 * W_A + W_B:]
    v = a_val * jax.nn.sigmoid(a_glu)
    v, new_a = _causal_dwconv(v, buf_a, w_dw_a, b_dw_a)
    v = jax.nn.silu(_layernorm(v, ln_g, ln_b)) * jax.nn.silu(a_gate)
    xb, new_b = _causal_dwconv(b_x, buf_b, w_conv_b, b_conv_b)
    yb, h_last = _rglru(xb, h0, w_r, b_r, w_i, b_i, lam)
    yb = yb * jax.nn.silu(b_gate)
    mix = jnp.concatenate([v, yb], axis=-1) @ w_out.astype(u.dtype)
    h = h + mix
    pe = (p.astype(h.dtype) @ w_pe.astype(h.dtype)) * jax.nn.sigmoid(h @ w_pg.astype(h.dtype))
    h = h + pe
    return h, new_a, new_b, h_last.astype(h0.dtype)


def _trunk(x, p, bufs_a, bufs_b, hs0, g_norm, w_in, w_dw_a, b_dw_a, ln_g, ln_b,
           w_conv_b, b_conv_b, w_r, b_r, w_i, b_i, lam, w_out, w_pe, w_pg, g_final):
    h = x
    na, nb, nh = [], [], []
    for l in range(DEPTH):
        h, a_, b_, h_ = _layer(h, p[l], bufs_a[l], bufs_b[l], hs0[l], g_norm[l], w_in[l],
                               w_dw_a[l], b_dw_a[l], ln_g[l], ln_b[l], w_conv_b[l], b_conv_b[l],
                               w_r[l], b_r[l], w_i[l], b_i[l], lam[l], w_out[l], w_pe[l], w_pg[l])
        na.append(a_)
        nb.append(b_)
        nh.append(h_)
    y = _rmsnorm(h, g_final)
    return y, jnp.stack(na, 0), jnp.stack(nb, 0), jnp.stack(nh, 0)


def setup_inputs(seed: int = 0) -> dict:
    key = jax.random.key(seed)
    ks = jax.random.split(key, 24)
    f32 = jnp.float32
    nrm = lambda k, s, sc: jax.random.normal(k, s, f32) * sc
    u = jax.random.uniform(ks[20], (DEPTH, W_B), f32, 0.9, 0.999)
    s = u ** (1.0 / RG_C)
    lam = jnp.log(s) - jnp.log1p(-s)
    return {
        "x_prompt": nrm(ks[0], (BATCH, SEQ, D_MODEL), 1.0),
        "x_sample": nrm(ks[1], (DEC_BATCH, DEC_SEQ, D_MODEL), 1.0),
        "p_prompt": nrm(ks[2], (DEPTH, BATCH, SEQ, PLE_DIM), 1.0),
        "p_sample": nrm(ks[3], (DEPTH, DEC_BATCH, DEC_SEQ, PLE_DIM), 1.0),
        "state_conv_a": nrm(ks[4], (DEPTH, DEC_BATCH, CONV_A_WIDTH - 1, W_A), 0.5),
        "state_conv_b": nrm(ks[5], (DEPTH, DEC_BATCH, CONV_B_WIDTH - 1, W_B), 1.0),
        "state_h": nrm(ks[6], (DEPTH, DEC_BATCH, W_B), 0.5),
        "g_norm": 1.0 + nrm(ks[7], (DEPTH, D_MODEL), 0.02),
        "w_in": nrm(ks[8], (DEPTH, D_MODEL, W_IN), D_MODEL ** -0.5),
        "w_dw_a": nrm(ks[9], (DEPTH, CONV_A_WIDTH, W_A), CONV_A_WIDTH ** -0.5),
        "b_dw_a": nrm(ks[10], (DEPTH, W_A), 0.02),
        "ln_g": 1.0 + nrm(ks[11], (DEPTH, W_A), 0.02),
        "ln_b": nrm(ks[12], (DEPTH, W_A), 0.02),
        "w_conv_b": nrm(ks[13], (DEPTH, CONV_B_WIDTH, W_B), CONV_B_WIDTH ** -0.5),
        "b_conv_b": nrm(ks[14], (DEPTH, W_B), 0.02),
        "w_r": nrm(ks[15], (DEPTH, N_B_HEADS, HEAD_B, HEAD_B), HEAD_B ** -0.5),
        "b_r": nrm(ks[16], (DEPTH, W_B), 0.02),
        "w_i": nrm(ks[17], (DEPTH, N_B_HEADS, HEAD_B, HEAD_B), HEAD_B ** -0.5),
        "b_i": nrm(ks[18], (DEPTH, W_B), 0.02),
        "lam": lam,
        "w_out": nrm(ks[19], (DEPTH, W_MIX, D_MODEL), W_MIX ** -0.5),
        "w_pe": nrm(ks[21], (DEPTH, PLE_DIM, D_MODEL), PLE_DIM ** -0.5),
        "w_pg": nrm(ks[22], (DEPTH, D_MODEL, D_MODEL), D_MODEL ** -0.5),
        "g_final": 1.0 + nrm(ks[23], (D_MODEL,), 0.02),
    }


def reference(x_prompt, x_sample, p_prompt, p_sample, state_conv_a, state_conv_b, state_h,
              g_norm, w_in, w_dw_a, b_dw_a, ln_g, ln_b, w_conv_b, b_conv_b,
              w_r, b_r, w_i, b_i, lam, w_out, w_pe, w_pg, g_final):
    weights = (g_norm, w_in, w_dw_a, b_dw_a, ln_g, ln_b, w_conv_b, b_conv_b,
               w_r, b_r, w_i, b_i, lam, w_out, w_pe, w_pg, g_final)
    bp = x_prompt.shape[0]
    za = jnp.zeros((DEPTH, bp, CONV_A_WIDTH - 1, W_A), x_prompt.dtype)
    zb = jnp.zeros((DEPTH, bp, CONV_B_WIDTH - 1, W_B), x_prompt.dtype)
    zh = jnp.zeros((DEPTH, bp, W_B), state_h.dtype)
    y_prompt, na_p, nb_p, nh_p = _trunk(x_prompt, p_prompt, za, zb, zh, *weights)
    y_sample, na_s, nb_s, nh_s = _trunk(x_sample, p_sample, state_conv_a, state_conv_b, state_h, *weights)
    return (y_prompt, y_sample, na_p, nb_p, nh_p, na_s, nb_s, nh_s)
```

```python
from contextlib import ExitStack

import numpy as np
import concourse.bass as bass
import concourse.mybir as mybir
from concourse.bass_utils import run_bass_kernel_spmd

F32 = mybir.dt.float32
BF16 = mybir.dt.bfloat16
AF = mybir.ActivationFunctionType
ALU = mybir.AluOpType

NCORES = 8
D = 1024
SEQ = 2048
NS = 16
KA = 31
KB = 4
EPS = 1e-6
TN = 512
RING = 28
NTMP = 11
NHALF = 8
LOOKAHEAD = 24
ZIPK = 5

ENGINES = ("pe", "act", "dve", "pool", "sp")


class Buf:
    __slots__ = ("name", "last_w", "readers", "sem", "dma_count")

    def __init__(self, name):
        self.name = name
        self.last_w = None
        self.readers = []
        self.sem = None
        self.dma_count = 0


class Op:
    __slots__ = ("eng", "fn", "reads", "writes", "deps", "is_dma", "key", "flag", "semval", "idx")

    def __init__(self, eng, fn, reads, writes, is_dma, key):
        self.eng = eng
        self.fn = fn
        self.reads = reads
        self.writes = writes
        self.is_dma = is_dma
        self.key = key
        self.deps = []
        self.flag = False
        self.semval = 0


class Prog:
    def __init__(self, nc):
        self.nc = nc
        self.ops = []
        self.dma_keys = []

    def add(self, eng, fn, reads=(), writes=(), dma_key=None):
        is_dma = dma_key is not None
        op = Op(eng, fn, list(reads), list(writes), is_dma, dma_key)
        op.idx = len(self.ops)
        deps = {}
        for b in op.reads:
            w = b.last_w
            if w is not None:
                deps[w.idx] = (w, True)
        for b in op.writes:
            w = b.last_w
            if w is not None and w.idx not in deps:
                deps[w.idx] = (w, False)
            for r in b.readers:
                if r.idx not in deps:
                    deps[r.idx] = (r, False)
        for b in op.reads:
            b.readers.append(op)
        for b in op.writes:
            b.last_w = op
            b.readers = []
        for (d, raw) in deps.values():
            if d is op:
                continue
            if d.is_dma or op.is_dma or d.eng != op.eng:
                op.deps.append(d)
            elif op.eng != "pe":
                op.deps.append(d)
        if is_dma and dma_key not in self.dma_keys:
            self.dma_keys.append(dma_key)
        self.ops.append(op)
        return op

    def emit(self):
        nc = self.nc
        for op in self.ops:
            for d in op.deps:
                d.flag = True
        cnt = {e: 0 for e in ENGINES}
        for op in self.ops:
            if op.is_dma:
                op.key.dma_count += 16
                op.semval = op.key.dma_count
            elif op.flag:
                cnt[op.eng] += 1
                op.semval = cnt[op.eng]
        with ExitStack() as es:
            esem = {e: es.enter_context(nc.semaphore("s_" + e)) for e in ENGINES if e != "sp"}
            for k in self.dma_keys:
                k.sem = es.enter_context(nc.semaphore("d_" + k.name))
            block = es.enter_context(nc.Block())
            per_eng = {e: [o for o in self.ops if o.eng == e] for e in ENGINES}
            final_dma = [(k.sem, k.dma_count) for k in self.dma_keys]

            def run(engname, eng):
                waited = {}
                for op in per_eng[engname]:
                    need = {}
                    for d in op.deps:
                        s = d.key.sem if d.is_dma else esem[d.eng]
                        if need.get(s, 0) < d.semval:
                            need[s] = d.semval
                    for s, v in need.items():
                        if waited.get(s, 0) < v:
                            eng.wait_ge(s, v)
                            waited[s] = v
                    ins = op.fn(eng)
                    if op.is_dma:
                        ins.then_inc(op.key.sem, 16)
                    elif op.flag:
                        ins.then_inc(esem[op.eng], 1)
                if engname == "sp":
                    for s, v in final_dma:
                        if waited.get(s, 0) < v:
                            eng.wait_ge(s, v)

            @block.tensor
            def _(e):
                run("pe", e)

            @block.scalar
            def _(e):
                run("act", e)

            @block.vector
            def _(e):
                run("dve", e)

            @block.gpsimd
            def _(e):
                run("pool", e)

            @block.sync
            def _(e):
                run("sp", e)


class Slot:
    def __init__(self, t, buf, nbytes):
        self.t = t
        self.b = buf
        self.nbytes = nbytes

    def f32(self):
        return self.t[:]

    def bf(self):
        return self.t[:].bitcast(BF16)


class Pool_:
    def __init__(self, slots):
        self.slots = slots
        self.i = 0

    def next(self):
        s = self.slots[self.i % len(self.slots)]
        self.i += 1
        return s


def build_nc():
    nc = bass.Bass("TRN2", target_bir_lowering=False)

    def din(name, shape):
        return nc.dram_tensor(name, shape, F32, kind="ExternalInput").ap()

    def dout(name, shape):
        return nc.dram_tensor(name, shape, F32, kind="ExternalOutput").ap()

    x_p = din("x_p", [SEQ, D])
    p_p = din("p_p", [SEQ, 256])
    x_s = din("x_s", [NS, D])
    p_s = din("p_s", [NS, 256])
    sca = din("sca", [NS * 30, D])
    scb = din("scb", [NS * 3, D])
    sh = din("sh", [NS, D])
    w_in = din("w_in", [40, 128, 1024])
    w_out = din("w_out", [16, 128, 1024])
    w_pg = din("w_pg", [8, 128, 1024])
    w_pe = din("w_pe", [2, 128, 1024])
    w_r = din("w_r", [128, 1024])
    w_i = din("w_i", [128, 1024])
    vecs = din("vecs", [128, 64])
    wdw = din("wdw", [128, 8 * KA])
    wcb = din("wcb", [128, 8 * KB])
    gfb = din("gfb", [128, D])
    ident = din("ident", [128, 128])

    y_p = dout("y_p", [SEQ, D])
    y_s = dout("y_s", [NS, D])
    na_p = dout("na_p", [30, D])
    nb_p = dout("nb_p", [3, D])
    nh_p = dout("nh_p", [1, D])
    na_s = dout("na_s", [NS, 30, D])
    nb_s = dout("nb_s", [NS, 3, D])
    nh_s = dout("nh_s", [NS, D])

    wscr = nc.dram_tensor("wscr", [66, 128, 1024], BF16).ap()
    dscr = nc.dram_tensor("dscr", [8, 128, KA * 128], BF16).ap()
    b_wscr = [Buf(f"wscr{j}") for j in range(66)]
    b_dscr = [Buf(f"dscr{j}") for j in range(8)]

    P = Prog(nc)
    uid = [0]

    def sb(shape, dt, name=None):
        uid[0] += 1
        return nc.alloc_sbuf_tensor(name or f"t{uid[0]}", shape, dt)

    def mkslots(n, prefix):
        return [Slot(sb([128, 512], F32, f"{prefix}{i}"), Buf(f"{prefix}{i}"), 2048) for i in range(n)]

    class FreePool:
        def __init__(self, slots, name):
            self.free = list(slots)
            self.name = name

        def alloc(self):
            assert self.free, f"pool {self.name} exhausted"
            return self.free.pop(0)

        def release(self, s):
            assert s not in self.free
            self.free.append(s)

    ring = [Slot(sb([128, 1024], BF16, f"ring{i}"), Buf(f"ring{i}"), 2048) for i in range(RING)]
    tpool = FreePool(mkslots(NTMP, "tmp"), "tmp")
    hpool = FreePool([Slot(sb([128, 512], BF16, f"hb{i}"), Buf(f"hb{i}"), 1024) for i in range(NHALF)], "half")
    rowpool = FreePool([Slot(sb([128, 1024], F32, f"row{i}"), Buf(f"row{i}"), 4096) for i in range(5)], "row")
    psG = FreePool([Slot(nc.alloc_psum_tensor(f"psg{i}", [128, 512], F32), Buf(f"psg{i}"), 2048) for i in range(6)], "psum")
    psS0 = Slot(nc.alloc_psum_tensor("pss0", [128, 512], F32), Buf("pss0"), 2048)
    psS1 = Slot(nc.alloc_psum_tensor("pss1", [128, 512], F32), Buf("pss1"), 2048)
    pinpool = FreePool([Slot(sb([128, 256], F32, f"pin{i}"), Buf(f"pin{i}"), 1024) for i in range(2)], "pin")

    cv = sb([128, 8, TN], F32, "cv")
    b_cv = [Buf(f"cv{c}") for c in range(8)]
    mixT = sb([128, 16, TN], BF16, "mixT")
    b_mix = [Buf(f"mix{c}") for c in range(16)]
    uT = sb([128, 8, TN], BF16, "uT")
    b_uT = Buf("uT")
    pTs = [sb([128, 2, TN], BF16, f"pT{i}") for i in range(2)]
    b_pTs = [Buf("pT0"), Buf("pT1")]
    uT_s = sb([128, 8, NS], BF16, "uT_s")
    b_uT_s = Buf("uT_s")
    mixT_s = sb([128, 16, NS], BF16, "mixT_s")
    b_mix_s = [Buf(f"mixs{c}") for c in range(16)]
    pT_s = sb([128, 2, NS], BF16, "pT_s")
    b_pT_s = Buf("pT_s")
    z_s = sb([128, 40, NS], F32, "z_s")
    b_zs = [Buf(f"zs{j}") for j in range(40)]
    vbuf = sb([128, 8, 30 + TN], BF16, "vbuf")
    b_vb = [Buf(f"vb{c}") for c in range(8)]
    bxbuf = sb([128, 8, 3 + TN], BF16, "bxbuf")
    b_bb = [Buf(f"bb{c}") for c in range(8)]
    diagA = [sb([128, KA, 128], BF16, f"diagA{i}") for i in range(2)]
    b_dA = [Buf("dA0"), Buf("dA1")]
    b_dAh = [Buf("dA0h"), Buf("dA1h")]
    KH = 16
    diagB = [sb([128, KB, 128], BF16, f"diagB{i}") for i in range(2)]
    b_dB = [Buf("dB0"), Buf("dB1")]
    wr_sb = sb([128, 1024], BF16, "wr_sb")
    wi_sb = sb([128, 1024], BF16, "wi_sb")
    b_wr = Buf("wr")
    b_wi = Buf("wi")
    vecs_sb = sb([128, 8, 8], F32, "vecs_sb")
    b_vecs = Buf("vecs")
    der = sb([128, 10, 8], F32, "der")
    b_der = Buf("der")
    wdw_sb = sb([128, 8, KA], F32, "wdw_sb")
    b_wdw = Buf("wdw")
    hwdw = sb([128, 8, KA], F32, "hwdw")
    b_hwdw = Buf("hwdw")
    wcb_sb = sb([128, 8, KB], F32, "wcb_sb")
    b_wcb = Buf("wcb")
    gfb_sb = sb([128, D], F32, "gfb_sb")
    b_gfb = Buf("gfb")
    identf = sb([128, 128], F32, "identf")
    b_idf = Buf("identf")
    identb = sb([128, 128], BF16, "identb")
    b_idb = Buf("identb")
    onesb = sb([128, 128], BF16, "onesb")
    b_ones = Buf("onesb")
    mean_sb = sb([128, TN], F32, "mean_sb")
    b_mean = Buf("mean")
    rstd_ln = sb([128, TN], F32, "rstd_ln")
    b_rln = Buf("rstdln")
    hstate = sb([128, 8], F32, "hstate")
    b_hsts = [Buf(f"hstate{c}") for c in range(8)]
    shT = sb([128, 8, NS], F32, "shT")
    b_shT = Buf("shT")
    stA = sb([128, 8, 30], F32, "stA")
    stB = sb([128, 8, NS], F32, "stB")
    stH = sb([128, 8, NS], F32, "stH")
    b_stA, b_stB, b_stH = Buf("stA"), Buf("stB"), Buf("stH")
    smalls = sb([128, 32], F32, "smalls")
    b_nbs = Buf("nbs")
    b_ssq = Buf("ssq")
    b_rstd = Buf("rstd")
    b_sq2s = [Buf(f"sq2_{i}") for i in range(4)]

    V_GN, V_BDW, V_LNG, V_LNB, V_BCB, V_BR, V_BI, V_LAM = range(8)
    D_HG, D_HB, D_HBR, D_HBI, D_S1, D_S2, D_T0, D_T1, D_T2, D_T3 = range(10)

    def vcol(i, c):
        return vecs_sb[:, i, c:c + 1]

    def dcol(i, c):
        return der[:, i, c:c + 1]

    def dma_in(dst_ap, src_ap, buf, eng="sp"):
        P.add(eng, lambda e: e.dma_start(out=dst_ap, in_=src_ap), writes=[buf], dma_key=buf)

    dma_in(vecs_sb[:].rearrange("p a b -> p (a b)"), vecs, b_vecs)
    dma_in(wdw_sb[:].rearrange("p a b -> p (a b)"), wdw, b_wdw)
    dma_in(wcb_sb[:].rearrange("p a b -> p (a b)"), wcb, b_wcb)
    dma_in(gfb_sb[:], gfb, b_gfb)
    dma_in(identf[:], ident, b_idf)
    dma_in(wr_sb[:], w_r, b_wr, eng="pool")
    dma_in(wi_sb[:], w_i, b_wi, eng="pool")

    P.add("dve", lambda e: e.tensor_copy(out=identb[:], in_=identf[:]), reads=[b_idf], writes=[b_idb])
    P.add("dve", lambda e: e.memset(onesb[:], 1.0 / 1024.0), writes=[b_ones])
    P.add("dve", lambda e: e.tensor_scalar(out=hwdw[:], in0=wdw_sb[:], scalar1=0.5, scalar2=None, op0=ALU.mult),
          reads=[b_wdw], writes=[b_hwdw])
    P.add("dve", lambda e: e.tensor_scalar(out=der[:, D_HG:D_HB + 1, :], in0=vecs_sb[:, V_LNG:V_LNB + 1, :], scalar1=0.5,
                                           scalar2=None, op0=ALU.mult), reads=[b_vecs], writes=[b_der])
    P.add("dve", lambda e: e.tensor_scalar(out=der[:, D_HBR:D_HBI + 1, :], in0=vecs_sb[:, V_BR:V_BI + 1, :], scalar1=0.5,
                                           scalar2=None, op0=ALU.mult), reads=[b_vecs], writes=[b_der])
    P.add("act", lambda e: e.activation(out=der[:, D_T0, :], in_=vecs_sb[:, V_LAM, :], func=AF.Abs),
          reads=[b_vecs], writes=[b_der])
    P.add("act", lambda e: e.activation(out=der[:, D_T1, :], in_=der[:, D_T0, :], func=AF.Exp, scale=-1.0),
          reads=[b_der], writes=[b_der])
    P.add("act", lambda e: e.activation(out=der[:, D_T2, :], in_=der[:, D_T1, :], func=AF.Ln, bias=1.0),
          reads=[b_der], writes=[b_der])
    P.add("dve", lambda e: e.tensor_scalar(out=der[:, D_T3, :], in0=vecs_sb[:, V_LAM, :], scalar1=-1.0, scalar2=0.0,
                                           op0=ALU.mult, op1=ALU.max), reads=[b_vecs], writes=[b_der])
    P.add("dve", lambda e: e.tensor_tensor(out=der[:, D_T3, :], in0=der[:, D_T3, :], in1=der[:, D_T2, :], op=ALU.add),
          reads=[b_der], writes=[b_der])
    P.add("dve", lambda e: e.tensor_scalar(out=der[:, D_S1, :], in0=der[:, D_T3, :], scalar1=-4.0, scalar2=None,
                                           op0=ALU.mult), reads=[b_der], writes=[b_der])
    P.add("dve", lambda e: e.tensor_scalar(out=der[:, D_S2, :], in0=der[:, D_T3, :], scalar1=-8.0, scalar2=None,
                                           op0=ALU.mult), reads=[b_der], writes=[b_der])

    def ab_schedule(sample=False):
        sched = []
        if sample:
            for nm in ("S0", "A1", "S1", "A2", "S2a", "S3", "S2b", "S4"):
                for c in range(8):
                    sched.append((nm, c))
            return sched
        for j in range(8 + 2):
            if j == 8:
                sched.append(("PRESQ", 0))
            if j == 9:
                sched.append(("PREEL", 0))
            if j == 0:
                sched.append(("S0", 0))
            if j + 1 < 8:
                sched.append(("S0", j + 1))
            if 0 <= j - 2 < 8:
                sched.append(("A2", j - 2))
            if j < 8:
                sched.append(("S1", j))
            if 0 <= j - 1 < 8:
                sched.append(("A1", j - 1))
            if j < 8:
                sched.append(("S2a", j))
                sched.append(("S2b", j))
                sched.append(("S3", j))
                sched.append(("S4", j))
        return sched

    def mk_pieces(sample):
        tp_ = []
        for (st, c) in ab_schedule(sample):
            if st == "A1":
                tp_ += [c, 8 + c]
            elif st == "S0":
                tp_ += [24 + c]
            elif st == "S3":
                tp_ += [32 + c]
        tp_ += [16 + c for c in range(8)]
        tp_ += [40 + k for k in range(16)]
        tp_ += [56 + k for k in range(8)]
        tp_ += [64 + k for k in range(2)]
        assert sorted(tp_) == list(range(66))
        return tp_

    def piece_src(j):
        if j < 40:
            return w_in[j]
        if j < 56:
            return w_out[j - 40]
        if j < 64:
            return w_pg[j - 56]
        return w_pe[j - 64]

    all_pieces = mk_pieces(False) * (SEQ // TN)
    rstate = {"issued": 0, "used": 0, "released": 0}

    def ring_pump(upto):
        while rstate["issued"] < len(all_pieces) and rstate["issued"] <= upto \
                and rstate["issued"] < rstate["released"] + RING:
            m = rstate["issued"]
            sl = ring[m % RING]
            j = all_pieces[m]
            if m < 66:
                src = piece_src(j)
                P.add("pool", lambda e, sl=sl, src=src: e.dma_start(out=sl.t[:], in_=src), writes=[sl.b],
                      dma_key=b_rst[m % RING])
                P.add("sp", lambda e, sl=sl, j=j: e.dma_start(out=wscr[j], in_=sl.t[:]), reads=[sl.b],
                      writes=[b_wscr[j]], dma_key=sl.b)
            else:
                P.add("sp", lambda e, sl=sl, j=j: e.dma_start(out=sl.t[:], in_=wscr[j]), reads=[b_wscr[j]],
                      writes=[sl.b], dma_key=sl.b)
            rstate["issued"] += 1

    rflags = [False] * len(all_pieces)
    b_rst = [Buf(f"rst{i}") for i in range(RING)]

    def use_piece(j_expect):
        n = rstate["used"]
        assert all_pieces[n] == j_expect, (n, all_pieces[n], j_expect)
        rstate["used"] += 1
        ring_pump(n + LOOKAHEAD)
        assert rstate["issued"] > n, "ring deadlock"
        sl = ring[n % RING]
        sl.cur = n
        return sl

    def release(slots):
        for sl in slots:
            rflags[sl.cur] = True
        while rstate["released"] < len(all_pieces) and rflags[rstate["released"]]:
            rstate["released"] += 1
        ring_pump(rstate["used"] - 1 + LOOKAHEAD)

    def state_out(src, b_src, W, scale, dst_ap):
        pa, pb = psG.alloc(), psG.alloc()
        for c in range(8):
            ps = pa if c < 4 else pb
            col = (c % 4) * 128
            P.add("pe", lambda e, ps=ps, col=col, c=c: e.transpose(out=ps.t[:W, col:col + 128], in_=src[:, c, :W],
                                                                    identity=identf[:]),
                  reads=[b_src, b_idf], writes=[ps.b])
        row = rowpool.alloc()
        for hf, ps in enumerate((pa, pb)):
            P.add("act", lambda e, ps=ps, hf=hf, row=row: e.activation(out=row.t[:W, hf * 512:(hf + 1) * 512],
                                                                         in_=ps.t[:W, :], func=AF.Copy, scale=scale),
                  reads=[ps.b], writes=[row.b])
        P.add("sp", lambda e, row=row: e.dma_start(out=dst_ap, in_=row.t[:W, :]), reads=[row.b], dma_key=row.b)
        psG.release(pa)
        psG.release(pb)
        rowpool.release(row)

    class Tile:
        def __init__(self, idx, N, subs, xd, pd, yd, is_sample, is_first_prompt, is_last_prompt):
            self.idx = idx
            self.N = N
            self.subs = subs
            self.xd, self.pd, self.yd = xd, pd, yd
            self.is_sample = is_sample
            self.is_first = is_first_prompt
            self.is_last = is_last_prompt
            self.stride = NS if is_sample else 1
            self.pT = pT_s if is_sample else pTs[idx % 2]
            self.b_pT = b_pT_s if is_sample else b_pTs[idx % 2]
            self.uT = uT_s if is_sample else uT
            self.b_uT = b_uT_s if is_sample else b_uT
            self.mixT = mixT_s if is_sample else mixT
            self.b_mix = b_mix_s if is_sample else b_mix
            self.comp = None

        def pre(self):
            self.pre_sq()
            self.pre_elem()
            self.pre_trans()

        def pre_sq(self):
            N, subs, xd, pd = self.N, self.subs, self.xd, self.pd
            pT, b_pT = self.pT, self.b_pT
            ssq = smalls[:, 0:4]
            rstd = smalls[:, 8:12]
            xins = []
            for s, (r0, R) in enumerate(subs):
                xin = rowpool.alloc()
                xins.append(xin)
                P.add("sp", lambda e, xin=xin, r0=r0, R=R: e.dma_start(out=xin.t[:R, :], in_=xd[r0:r0 + R, :]),
                      writes=[xin.b], dma_key=xin.b)
                junk = tpool.alloc()
                P.add("act", lambda e, xin=xin, junk=junk, R=R, s=s: e.activation(
                    out=junk.bf()[:R, :], in_=xin.t[:R, :], func=AF.Square, accum_out=ssq[:R, s:s + 1]),
                    reads=[xin.b], writes=[junk.b, b_ssq])
                tpool.release(junk)
            R0 = subs[0][1]
            ns = len(subs)
            P.add("act", lambda e: e.activation(out=ssq[:R0, 0:ns], in_=ssq[:R0, 0:ns], func=AF.Sqrt, scale=1.0 / D, bias=EPS),
                  reads=[b_ssq], writes=[b_ssq])
            P.add("dve", lambda e: e.reciprocal(out=rstd[:R0, 0:ns], in_=ssq[:R0, 0:ns]), reads=[b_ssq], writes=[b_rstd])
            self.xins = xins

        def pre_elem(self):
            N, subs, xd, pd = self.N, self.subs, self.xd, self.pd
            pT, b_pT = self.pT, self.b_pT
            rstd = smalls[:, 8:12]
            xins = self.xins
            xn16s = []
            for s, (r0, R) in enumerate(subs):
                xn = tpool.alloc()
                xn16s.append(xn)
                xin = xins[s]
                P.add("act", lambda e, xn=xn, xin=xin, R=R, s=s: e.activation(
                    out=xn.bf()[:R, :], in_=xin.t[:R, :], func=AF.Copy, scale=rstd[:R, s:s + 1]),
                    reads=[xin.b, b_rstd], writes=[xn.b])
                rowpool.release(xin)
            self.xn16s = xn16s
            p16s = []
            for s, (r0, R) in enumerate(subs):
                pin = pinpool.alloc()
                P.add("sp", lambda e, pin=pin, r0=r0, R=R: e.dma_start(out=pin.t[:R, 0:256], in_=pd[r0:r0 + R, :]),
                      writes=[pin.b], dma_key=pin.b)
                p16 = hpool.alloc()
                p16s.append(p16)
                P.add("dve", lambda e, pin=pin, p16=p16, R=R: e.tensor_copy(out=p16.t[:R, 0:256], in_=pin.t[:R, 0:256]),
                      reads=[pin.b], writes=[p16.b])
                pinpool.release(pin)
            self.p16s = p16s

        def pre_trans(self):
            N, subs = self.N, self.subs
            pT, b_pT = self.pT, self.b_pT
            xn16s = self.xn16s
            p16s = self.p16s
            for j in range(2):
                pt = (psS0, psS1)[j]
                for s, (r0, R) in enumerate(subs):
                    p16 = p16s[s]
                    P.add("pe", lambda e, pt=pt, p16=p16, R=R, s=s, j=j: e.transpose(
                        out=pt.bf()[:, s * 128:s * 128 + R], in_=p16.t[:R, j * 128:(j + 1) * 128], identity=identb[:R, :R]),
                        reads=[p16.b, b_idb], writes=[pt.b])
                P.add("act", lambda e, pt=pt, j=j: e.activation(out=pT[:, j, :N], in_=pt.bf()[:, :N], func=AF.Copy),
                      reads=[pt.b], writes=[b_pT])
            for p16 in p16s:
                hpool.release(p16)
            for kc in range(8):
                pt = psG.alloc()
                for s, (r0, R) in enumerate(subs):
                    xn = xn16s[s]
                    P.add("pe", lambda e, pt=pt, xn=xn, R=R, s=s, kc=kc: e.transpose(
                        out=pt.bf()[:, s * 128:s * 128 + R], in_=xn.bf()[:R, kc * 128:(kc + 1) * 128], identity=identb[:R, :R]),
                        reads=[xn.b, b_idb], writes=[pt.b])
                P.add("dve", lambda e, pt=pt, kc=kc: e.tensor_scalar(
                    out=self.uT[:, kc, :N], in0=pt.bf()[:, :N], scalar1=vcol(V_GN, kc), scalar2=None, op0=ALU.mult),
                    reads=[pt.b, b_vecs], writes=[self.b_uT])
                psG.release(pt)
            for xn in xn16s:
                tpool.release(xn)

            if self.is_first:
                for c in range(8):
                    P.add("pool", lambda e, c=c: e.memset(vbuf[:, c, 0:30], 0.0), writes=[b_vb[c]])
                    P.add("pool", lambda e, c=c: e.memset(bxbuf[:, c, 0:3], 0.0), writes=[b_bb[c]])
                P.add("pool", lambda e: e.memset(hstate[:], 0.0), writes=b_hsts)

        def sample_state(self):
            grows = [rowpool.alloc() for _ in range(4)]
            for g in range(4):
                P.add("sp", lambda e, g=g: e.dma_start(out=grows[g].t[:120, :], in_=sca[g * 120:(g + 1) * 120, :]),
                      writes=[grows[g].b], dma_key=grows[g].b)
            P.add("sp", lambda e: e.dma_start(out=na_s[:, 0:29, :], in_=sca.rearrange("(s k) d -> s k d", k=30)[:, 1:30, :]),
                  dma_key=b_nbs)
            P.add("sp", lambda e: e.dma_start(out=nb_s[:, 0:2, :], in_=scb.rearrange("(s k) d -> s k d", k=3)[:, 1:3, :]),
                  dma_key=b_nbs)
            self.grows = grows

        def sample_state_trans(self):
            grows = self.grows
            for c in range(8):
                ps = psG.alloc()
                for g in range(4):
                    P.add("pe", lambda e, ps=ps, g=g, c=c: e.transpose(
                        out=ps.t[:, g * 120:(g + 1) * 120], in_=grows[g].t[:120, c * 128:(c + 1) * 128],
                        identity=identf[:120, :120]),
                        reads=[grows[g].b, b_idf], writes=[ps.b])
                P.add("act", lambda e, ps=ps, c=c: e.activation(
                    out=vbuf[:, c, 0:480].rearrange("p (k s) -> p k s", s=NS),
                    in_=ps.t[:, 0:480].rearrange("p (s k) -> p k s", k=30), func=AF.Copy, scale=2.0),
                    reads=[ps.b], writes=[b_vb[c]])
                psG.release(ps)
            for g in range(4):
                rowpool.release(grows[g])
            rowb = rowpool.alloc()
            P.add("sp", lambda e: e.dma_start(out=rowb.t[:48, :], in_=scb[:, :]), writes=[rowb.b], dma_key=rowb.b)
            for c in range(8):
                ps = psG.alloc()
                P.add("pe", lambda e, ps=ps, c=c: e.transpose(out=ps.t[:, 0:48], in_=rowb.t[:48, c * 128:(c + 1) * 128],
                                                             identity=identf[:48, :48]),
                      reads=[rowb.b, b_idf], writes=[ps.b])
                P.add("act", lambda e, ps=ps, c=c: e.activation(
                    out=bxbuf[:, c, 0:48].rearrange("p (k s) -> p k s", s=NS),
                    in_=ps.t[:, 0:48].rearrange("p (s k) -> p k s", k=3), func=AF.Copy),
                    reads=[ps.b], writes=[b_bb[c]])
                psG.release(ps)
            rowpool.release(rowb)
            rowh = rowpool.alloc()
            P.add("sp", lambda e: e.dma_start(out=rowh.t[:NS, :], in_=sh[:, :]), writes=[rowh.b], dma_key=rowh.b)
            ps = psG.alloc()
            for c in range(8):
                P.add("pe", lambda e, ps=ps, c=c: e.transpose(out=ps.t[:, c * NS:(c + 1) * NS],
                                                             in_=rowh.t[:NS, c * 128:(c + 1) * 128],
                                                             identity=identf[:NS, :NS]),
                      reads=[rowh.b, b_idf], writes=[ps.b])
            P.add("act", lambda e, ps=ps: e.activation(out=shT[:].rearrange("p a b -> p (a b)"), in_=ps.t[:, 0:8 * NS],
                                                       func=AF.Copy), reads=[ps.b], writes=[b_shT])
            psG.release(ps)
            rowpool.release(rowh)

        def inproj(self, ps, wsl, j):
            N = self.N
            for kc in range(8):
                P.add("pe", lambda e, kc=kc: e.matmul(ps.t[:, :N], lhsT=wsl.t[:, kc * 128:(kc + 1) * 128],
                                                      rhs=self.uT[:, kc, :N], start=(kc == 0), stop=(kc == 7)),
                      reads=[wsl.b, self.b_uT], writes=[ps.b])
            if self.comp is not None:
                pz = psG.alloc()
                for kc in range(8):
                    P.add("pe", lambda e, kc=kc: e.matmul(pz.t[:, :NS], lhsT=wsl.t[:, kc * 128:(kc + 1) * 128],
                                                          rhs=uT_s[:, kc, :], start=(kc == 0), stop=(kc == 7)),
                          reads=[wsl.b, b_uT_s], writes=[pz.b])
                P.add("dve", lambda e: e.tensor_copy(out=z_s[:, j, :], in_=pz.t[:, :NS]), reads=[pz.b], writes=[b_zs[j]])
                psG.release(pz)
            release([wsl])

        def zget(self, j):
            if self.is_sample:
                return Slot(z_s[:, j, :], b_zs[j], 64), False
            w = use_piece(j)
            ps = psG.alloc()
            self.inproj(ps, w, j)
            return ps, True

        def zput(self, ps, owned):
            if owned:
                psG.release(ps)

        def diag_load(self, c):
            P.add("sp", lambda e, c=c: e.dma_start(out=diagA[c % 2][:].rearrange("p a b -> p (a b)"), in_=dscr[c]),
                  reads=[b_dscr[c]], writes=[b_dA[c % 2], b_dAh[c % 2]], dma_key=b_dA[c % 2])

        def diag_build(self, c):
            dA = diagA[c % 2]
            P.add("dve", lambda e: e.tensor_tensor(
                out=dA[:, 0:KH, :], in0=identb[:].unsqueeze(1).to_broadcast([128, KH, 128]),
                in1=hwdw[:, c, 0:KH].unsqueeze(2).to_broadcast([128, KH, 128]), op=ALU.mult),
                reads=[b_idb, b_hwdw], writes=[b_dA[c % 2]])
            P.add("dve", lambda e: e.tensor_tensor(
                out=dA[:, KH:KA, :], in0=identb[:].unsqueeze(1).to_broadcast([128, KA - KH, 128]),
                in1=hwdw[:, c, KH:KA].unsqueeze(2).to_broadcast([128, KA - KH, 128]), op=ALU.mult),
                reads=[b_idb, b_hwdw], writes=[b_dAh[c % 2]])
            P.add("sp", lambda e: e.dma_start(out=dscr[c], in_=dA[:].rearrange("p a b -> p (a b)")),
                  reads=[b_dA[c % 2], b_dAh[c % 2]], writes=[b_dscr[c]], dma_key=b_dA[c % 2])

        def ab_gen(self):
            N, stride = self.N, self.stride
            is_sample, is_last = self.is_sample, self.is_last
            st = [dict() for _ in range(8)]
            self.pending = None
            if is_sample:
                bigs = [tpool.alloc(), tpool.alloc()]
                bigh = hpool.alloc()
                self.bigh = bigh
                def carve(big, width, nm):
                    out = []
                    for k in range(512 // width):
                        b = Buf(f"{nm}_{k}")
                        b.last_w = big.b.last_w
                        b.readers = list(big.b.readers)
                        out.append(Slot(big.t[:, k * width:(k + 1) * width], b, 64))
                    return out

                self.small = [(b_, carve(b_, NS, f"sm{bi}")) for bi, b_ in enumerate(bigs)]
                self.smallh = (bigh, carve(bigh, 2 * NS, "smh"))
                tp = FreePool([x for (_, lst) in self.small for x in lst], "smalltmp")
                hp = FreePool(list(self.smallh[1]), "smallhalf")
            else:
                tp, hp = tpool, hpool
            builds = (self.idx == 0)
            if is_sample:
                pass
            elif not builds:
                self.diag_load(0)
                self.diag_load(1)
            else:
                self.diag_build(0)
                self.diag_build(1)

            def A1(c):
                ps_av, o_av = self.zget(c)
                ps_ag, o_ag = self.zget(8 + c)
                th = tp.alloc()
                P.add("act", lambda e: e.activation(out=th.t[:, :N], in_=ps_ag.t[:, :N], func=AF.Tanh, scale=0.5),
                      reads=[ps_ag.b], writes=[th.b])
                o0 = 30 * stride
                P.add("dve", lambda e: e.scalar_tensor_tensor(
                    out=vbuf[:, c, o0:o0 + N], in0=th.t[:, :N], scalar=1.0, in1=ps_av.t[:, :N], op0=ALU.add, op1=ALU.mult),
                    reads=[th.b, ps_av.b], writes=[b_vb[c]])
                if is_sample or is_last:
                    W = NS if is_sample else 30
                    P.add("dve", lambda e: e.scalar_tensor_tensor(
                        out=stA[:, c, :W], in0=th.t[:, N - W:N], scalar=1.0, in1=ps_av.t[:, N - W:N], op0=ALU.add, op1=ALU.mult),
                        reads=[th.b, ps_av.b], writes=[b_stA])
                tp.release(th)
                self.zput(ps_av, o_av)
                self.zput(ps_ag, o_ag)

            def stats_mm(c, cv16, sq16):
                if is_sample:
                    P.add("pe", lambda e: e.matmul(psS1.t[:, :2 * N], lhsT=onesb[:], rhs=cv16.t[:, :2 * N], start=(c == 0), stop=(c == 7)),
                          reads=[cv16.b, b_ones], writes=[psS1.b])
                    hp.release(cv16)
                    return
                P.add("pe", lambda e: e.matmul(psS0.t[:, :N], lhsT=onesb[:], rhs=cv16.t[:, :N], start=(c == 0), stop=(c == 7)),
                      reads=[cv16.b, b_ones], writes=[psS0.b])
                P.add("pe", lambda e: e.matmul(psS1.t[:, :N], lhsT=onesb[:], rhs=sq16.t[:, :N], start=(c == 0), stop=(c == 7)),
                      reads=[sq16.b, b_ones], writes=[psS1.b])
                hp.release(cv16)
                hp.release(sq16)

            self.stats_mm = stats_mm

            def A2s(c):
                prod = tpool.alloc()
                pv = prod.t[:, 0:NS * KA].rearrange("p (s k) -> p s k", k=KA)
                P.add("dve", lambda e: e.tensor_tensor(
                    out=pv, in0=vbuf[:, c, 0:NS * KA].rearrange("p (k s) -> p s k", s=NS),
                    in1=hwdw[:, c, :].unsqueeze(1).to_broadcast([128, NS, KA]), op=ALU.mult),
                    reads=[b_vb[c], b_hwdw], writes=[prod.b])
                cvp = tp.alloc()
                P.add("dve", lambda e: e.tensor_reduce(out=cvp.t[:, :N], in_=pv, axis=mybir.AxisListType.X, op=ALU.add),
                      reads=[prod.b], writes=[cvp.b])
                tpool.release(prod)
                if self.pending is not None:
                    stats_mm(*self.pending)
                P.add("act", lambda e: e.activation(out=cv[:, c, :N], in_=cvp.t[:, :N], func=AF.Identity, bias=vcol(V_BDW, c)),
                      reads=[cvp.b, b_vecs], writes=[b_cv[c]])
                tp.release(cvp)
                cs16 = hp.alloc()
                P.add("dve", lambda e: e.tensor_tensor(out=cs16.t[:, N:2 * N], in0=cv[:, c, :N], in1=cv[:, c, :N], op=ALU.mult),
                      reads=[b_cv[c]], writes=[cs16.b])
                P.add("dve", lambda e: e.tensor_copy(out=cs16.t[:, :N], in_=cv[:, c, :N]), reads=[b_cv[c]], writes=[cs16.b])
                self.pending = (c, cs16, cs16)

            def A2(c):
                if is_sample:
                    return A2s(c)
                dA = diagA[c % 2]
                bdA = b_dA[c % 2]
                ps = psG.alloc()
                for k in range(KA):
                    P.add("pe", lambda e, k=k: e.matmul(
                        ps.t[:, :N], lhsT=dA[:, k, :], rhs=vbuf[:, c, k * stride:k * stride + N], start=(k == 0), stop=(k == KA - 1)),
                        reads=[bdA if k < KH else b_dAh[c % 2], b_vb[c]], writes=[ps.b])
                if c + 2 < 8:
                    if builds:
                        self.diag_build(c + 2)
                    else:
                        self.diag_load(c + 2)
                if self.pending is not None:
                    stats_mm(*self.pending)
                P.add("act", lambda e: e.activation(out=cv[:, c, :N], in_=ps.t[:, :N], func=AF.Identity, bias=vcol(V_BDW, c)),
                      reads=[ps.b, b_vecs], writes=[b_cv[c]])
                sq16 = hp.alloc()
                so = N if is_sample else 0
                P.add("dve", lambda e: e.scalar_tensor_tensor(
                    out=sq16.t[:, so:so + N], in0=ps.t[:, :N], scalar=vcol(V_BDW, c), in1=cv[:, c, :N], op0=ALU.add, op1=ALU.mult),
                    reads=[ps.b, b_vecs, b_cv[c]], writes=[sq16.b])
                psG.release(ps)
                cv16 = sq16 if is_sample else hp.alloc()
                P.add("dve", lambda e: e.tensor_copy(out=cv16.t[:, :N], in_=cv[:, c, :N]), reads=[b_cv[c]], writes=[cv16.b])
                self.pending = (c, cv16, sq16)
                if not is_sample and not is_last:
                    P.add("pool", lambda e: e.tensor_copy(out=vbuf[:, c, 0:30], in_=vbuf[:, c, TN:TN + 30]),
                          reads=[b_vb[c]], writes=[b_vb[c]])

            def S0(c):
                ps, o_bx = self.zget(24 + c)
                o0 = 3 * stride
                P.add("act", lambda e: e.activation(out=bxbuf[:, c, o0:o0 + N], in_=ps.t[:, :N], func=AF.Copy),
                      reads=[ps.b], writes=[b_bb[c]])
                if is_sample or is_last:
                    W = NS if is_sample else 3
                    P.add("act", lambda e: e.activation(out=stB[:, c, :W], in_=ps.t[:, N - W:N], func=AF.Copy),
                          reads=[ps.b], writes=[b_stB])
                self.zput(ps, o_bx)

            def build_dB(c):
                dB = diagB[c % 2]
                P.add("pool", lambda e: e.tensor_tensor(
                    out=dB[:], in0=identb[:].unsqueeze(1).to_broadcast([128, KB, 128]),
                    in1=wcb_sb[:, c, :].unsqueeze(2).to_broadcast([128, KB, 128]), op=ALU.mult),
                    reads=[b_idb, b_wcb], writes=[b_dB[c % 2]])

            build_dB(0)
            build_dB(1)

            def S1(c):
                dB = diagB[c % 2]
                bdB = b_dB[c % 2]
                ps = psG.alloc()
                for k in range(KB):
                    P.add("pe", lambda e, k=k: e.matmul(
                        ps.t[:, :N], lhsT=dB[:, k, :], rhs=bxbuf[:, c, k * stride:k * stride + N], start=(k == 0), stop=(k == KB - 1)),
                        reads=[bdB, b_bb[c]], writes=[ps.b])
                if c + 2 < 8:
                    build_dB(c + 2)
                if not is_sample and not is_last:
                    P.add("pool", lambda e: e.tensor_copy(out=bxbuf[:, c, 0:3], in_=bxbuf[:, c, TN:TN + 3]),
                          reads=[b_bb[c]], writes=[b_bb[c]])
                xb16 = hp.alloc()
                P.add("act", lambda e: e.activation(out=xb16.t[:, :N], in_=ps.t[:, :N], func=AF.Identity, bias=vcol(V_BCB, c)),
                      reads=[ps.b, b_vecs], writes=[xb16.b])
                xb = tp.alloc()
                P.add("act", lambda e: e.activation(out=xb.t[:, :N], in_=ps.t[:, :N], func=AF.Identity, bias=vcol(V_BCB, c)),
                      reads=[ps.b, b_vecs], writes=[xb.b])
                psG.release(ps)
                st[c]["xb"] = xb
                st[c]["xb16"] = xb16

            def S2a(c):
                xb16 = st[c]["xb16"]
                ps_r, ps_i = psG.alloc(), psG.alloc()
                P.add("pe", lambda e: e.matmul(ps_r.t[:, :N], lhsT=wr_sb[:, c * 128:(c + 1) * 128], rhs=xb16.t[:, :N],
                                               start=True, stop=True), reads=[b_wr, xb16.b], writes=[ps_r.b])
                P.add("pe", lambda e: e.matmul(ps_i.t[:, :N], lhsT=wi_sb[:, c * 128:(c + 1) * 128], rhs=xb16.t[:, :N],
                                               start=True, stop=True), reads=[b_wi, xb16.b], writes=[ps_i.b])
                hp.release(xb16)
                thr, thi = tp.alloc(), tp.alloc()
                P.add("act", lambda e: e.activation(out=thr.t[:, :N], in_=ps_r.t[:, :N], func=AF.Tanh, scale=0.5,
                                                    bias=dcol(D_HBR, c)), reads=[ps_r.b, b_der], writes=[thr.b])
                P.add("act", lambda e: e.activation(out=thi.t[:, :N], in_=ps_i.t[:, :N], func=AF.Tanh, scale=0.5,
                                                    bias=dcol(D_HBI, c)), reads=[ps_i.b, b_der], writes=[thi.b])
                psG.release(ps_r)
                psG.release(ps_i)
                a_, a2 = tp.alloc(), tp.alloc()
                P.add("act", lambda e: e.activation(out=a_.t[:, :N], in_=thr.t[:, :N], func=AF.Exp, scale=dcol(D_S1, c),
                                                    bias=dcol(D_S1, c)), reads=[thr.b, b_der], writes=[a_.b])
                P.add("act", lambda e: e.activation(out=a2.t[:, :N], in_=thr.t[:, :N], func=AF.Exp, scale=dcol(D_S2, c),
                                                    bias=dcol(D_S2, c)), reads=[thr.b, b_der], writes=[a2.b])
                tp.release(thr)
                st[c].update(thi=thi, a_=a_, a2=a2)

            def S2b(c):
                a2 = st[c]["a2"]
                P.add("act", lambda e: e.activation(out=a2.t[:, :N], in_=a2.t[:, :N], func=AF.Sqrt, scale=-1.0, bias=1.0),
                      reads=[a2.b], writes=[a2.b])

            def S3(c):
                ps_bg, o_bg = self.zget(32 + c)
                thg = tp.alloc()
                P.add("act", lambda e: e.activation(out=thg.t[:, :N], in_=ps_bg.t[:, :N], func=AF.Tanh, scale=0.5),
                      reads=[ps_bg.b], writes=[thg.b])
                P.add("dve", lambda e: e.scalar_tensor_tensor(
                    out=thg.t[:, :N], in0=thg.t[:, :N], scalar=1.0, in1=ps_bg.t[:, :N], op0=ALU.add, op1=ALU.mult),
                    reads=[thg.b, ps_bg.b], writes=[thg.b])
                self.zput(ps_bg, o_bg)
                st[c]["thg"] = thg

            def S4(c):
                xb, thi, a_, a2, thg = (st[c][k] for k in ("xb", "thi", "a_", "a2", "thg"))
                P.add("dve", lambda e: e.scalar_tensor_tensor(
                    out=thi.t[:, :N], in0=thi.t[:, :N], scalar=1.0, in1=xb.t[:, :N], op0=ALU.add, op1=ALU.mult),
                    reads=[thi.b, xb.b], writes=[thi.b])
                tp.release(xb)
                P.add("dve", lambda e: e.scalar_tensor_tensor(
                    out=thi.t[:, :N], in0=thi.t[:, :N], scalar=0.5, in1=a2.t[:, :N], op0=ALU.mult, op1=ALU.mult),
                    reads=[thi.b, a2.b], writes=[thi.b])
                tp.release(a2)
                hb_ = tp.alloc()
                if is_sample:
                    P.add("dve", lambda e: e.tensor_tensor(out=hb_.t[:, :N], in0=a_.t[:, :N], in1=shT[:, c, :], op=ALU.mult),
                          reads=[a_.b, b_shT], writes=[hb_.b])
                    P.add("dve", lambda e: e.tensor_tensor(out=hb_.t[:, :N], in0=hb_.t[:, :N], in1=thi.t[:, :N], op=ALU.add),
                          reads=[hb_.b, thi.b], writes=[hb_.b])
                    P.add("pool", lambda e: e.tensor_copy(out=stH[:, c, :NS], in_=hb_.t[:, :NS]),
                          reads=[hb_.b], writes=[b_stH])
                else:
                    P.add("dve", lambda e: e.tensor_tensor_scan(
                        out=hb_.t[:, :N], data0=a_.t[:, :N], data1=thi.t[:, :N], initial=hstate[:, c:c + 1],
                        op0=ALU.mult, op1=ALU.add), reads=[a_.b, thi.b, b_hsts[c]], writes=[hb_.b])
                    P.add("pool", lambda e: e.tensor_copy(out=hstate[:, c:c + 1], in_=hb_.t[:, N - 1:N]),
                          reads=[hb_.b], writes=[b_hsts[c]])
                    if is_last:
                        P.add("pool", lambda e: e.tensor_copy(out=stH[:, c, 0:1], in_=hb_.t[:, N - 1:N]),
                              reads=[hb_.b], writes=[b_stH])
                tp.release(a_)
                tp.release(thi)
                P.add("dve", lambda e: e.scalar_tensor_tensor(
                    out=self.mixT[:, 8 + c, :N], in0=hb_.t[:, :N], scalar=0.5, in1=thg.t[:, :N], op0=ALU.mult, op1=ALU.mult),
                    reads=[hb_.b, thg.b], writes=[self.b_mix[8 + c]])
                tp.release(hb_)
                tp.release(thg)

            fns = dict(A1=A1, A2=A2, S0=S0, S1=S1, S2a=S2a, S2b=S2b, S3=S3, S4=S4)
            def PRESQ(c):
                if self.nxt is not None:
                    self.nxt.pre_sq()

            def PREEL(c):
                if self.nxt is not None:
                    self.nxt.pre_elem()

            fns.update(PRESQ=PRESQ, PREEL=PREEL)
            for (name, c) in ab_schedule(is_sample):
                fns[name](c)
                yield name
                if name == "A1" and c == 7:
                    if is_sample:
                        state_out(stA, b_stA, NS, 0.5, na_s[:, 29, :])
                    elif is_last:
                        state_out(stA, b_stA, 30, 0.5, na_p[:, :])
                if name == "S0" and c == 7:
                    if is_sample:
                        state_out(stB, b_stB, NS, 1.0, nb_s[:, 2, :])
                    elif is_last:
                        state_out(stB, b_stB, 3, 1.0, nb_p[:, :])
            if is_sample:
                state_out(stH, b_stH, NS, 1.0, nh_s[:, :])
                for (b_, lst) in self.small:
                    for x in lst:
                        if x.b.last_w is not None:
                            b_.b.readers.append(x.b.last_w)
                        b_.b.readers.extend(x.b.readers)
                    tpool.release(b_)
            elif is_last:
                state_out(stH, b_stH, 1, 1.0, nh_p[:, :])

        def a3(self, nxt=None):
            N = self.N
            self.stats_mm(*self.pending)
            if self.is_sample:
                bh, lst = self.smallh
                for x in lst:
                    if x.b.last_w is not None:
                        bh.b.readers.append(x.b.last_w)
                    bh.b.readers.extend(x.b.readers)
                hpool.release(bh)
            m2 = tpool.alloc()
            pm = psS1 if self.is_sample else psS0
            mo = N if self.is_sample else 0
            P.add("act", lambda e: e.activation(out=mean_sb[:, :N], in_=pm.t[:, :N], func=AF.Copy), reads=[pm.b], writes=[b_mean])
            P.add("act", lambda e: e.activation(out=m2.t[:, :N], in_=pm.t[:, :N], func=AF.Square), reads=[pm.b], writes=[m2.b])
            P.add("dve", lambda e: e.tensor_tensor(out=m2.t[:, :N], in0=psS1.t[:, mo:mo + N], in1=m2.t[:, :N], op=ALU.subtract),
                  reads=[psS1.b, m2.b], writes=[m2.b])
            P.add("act", lambda e: e.activation(out=m2.t[:, :N], in_=m2.t[:, :N], func=AF.Sqrt, bias=EPS), reads=[m2.b], writes=[m2.b])
            P.add("dve", lambda e: e.reciprocal(out=rstd_ln[:, :N], in_=m2.t[:, :N]), reads=[m2.b], writes=[b_rln])
            tpool.release(m2)
            for c in range(8):
                P.add("pool", lambda e, c=c: e.tensor_tensor(out=cv[:, c, :N], in0=cv[:, c, :N], in1=mean_sb[:, :N], op=ALU.subtract),
                      reads=[b_cv[c], b_mean], writes=[b_cv[c]])
            for c in range(8):
                ps_g, o_g = self.zget(16 + c)
                t = tpool.alloc()
                P.add("dve", lambda e, t=t, c=c: e.tensor_tensor(out=t.t[:, :N], in0=cv[:, c, :N], in1=rstd_ln[:, :N], op=ALU.mult),
                      reads=[b_cv[c], b_rln], writes=[t.b])
                thl = tpool.alloc()
                P.add("act", lambda e, t=t, thl=thl, c=c: e.activation(out=thl.t[:, :N], in_=t.t[:, :N], func=AF.Silu,
                                                                       scale=vcol(V_LNG, c), bias=vcol(V_LNB, c)),
                      reads=[t.b, b_vecs], writes=[thl.b])
                thg = tpool.alloc()
                P.add("act", lambda e, thg=thg, ps=ps_g: e.activation(out=thg.t[:, :N], in_=ps.t[:, :N], func=AF.Silu),
                      reads=[ps_g.b], writes=[thg.b])
                tpool.release(t)
                self.zput(ps_g, o_g)
                P.add("dve", lambda e, thl=thl, thg=thg, c=c: e.tensor_tensor(
                    out=self.mixT[:, c, :N], in0=thl.t[:, :N], in1=thg.t[:, :N], op=ALU.mult),
                    reads=[thl.b, thg.b], writes=[self.b_mix[c]])
                tpool.release(thl)
                tpool.release(thg)

        def post_gen(self, hold=False, pieces=None):
            subs, xd, yd = self.subs, self.xd, self.yd
            pT, b_pT = self.pT, self.b_pT
            mixT_, b_mix_ = self.mixT, self.b_mix
            wide = self.is_sample
            if wide:
                P.add("pool", lambda e: e.tensor_copy(out=mixT[:, :, 0:NS], in_=mixT_s[:, :, :]),
                      reads=list(b_mix_s), writes=list(b_mix))
                mixT_, b_mix_ = mixT, b_mix
            if pieces is None:
                wout = [use_piece(40 + k) for k in range(16)]
                wpg = [use_piece(56 + k) for k in range(8)]
                wpe = [use_piece(64 + k) for k in range(2)]
            else:
                wout, wpg, wpe = pieces
            self.post_pieces = (wout, wpg, wpe)
            nsub = len(subs)
            sst = [dict() for _ in range(nsub)]

            def M(s):
                r0, R = subs[s]
                if getattr(self, "xr_pre", None) is not None:
                    xr = self.xr_pre
                else:
                    xr = rowpool.alloc()
                    P.add("sp", lambda e: e.dma_start(out=xr.t[:R, :], in_=xd[r0:r0 + R, :]), writes=[xr.b], dma_key=xr.b)
                psm = [psG.alloc(), psG.alloc()]
                for hf in range(2):
                    for kc in range(16):
                        P.add("pe", lambda e, ps=psm[hf], kc=kc, hf=hf: e.matmul(
                            ps.t[:(128 if wide else R), :], lhsT=mixT_[:, kc, s * 128:s * 128 + (128 if wide else R)],
                            rhs=wout[kc].t[:, hf * 512:(hf + 1) * 512], start=(kc == 0), stop=(kc == 15)),
                            reads=[b_mix_[kc], wout[kc].b], writes=[psm[hf].b])
                if s == nsub - 1 and not hold:
                    release(wout)
                for hf in range(2):
                    P.add("dve", lambda e, ps=psm[hf], hf=hf: e.tensor_tensor(
                        out=xr.t[:R, hf * 512:(hf + 1) * 512], in0=ps.t[:R, :], in1=xr.t[:R, hf * 512:(hf + 1) * 512], op=ALU.add),
                        reads=[psm[hf].b, xr.b], writes=[xr.b])
                    psG.release(psm[hf])
                h16 = tpool.alloc()
                P.add("act", lambda e: e.activation(out=h16.bf()[:R, :], in_=xr.t[:R, :], func=AF.Copy),
                      reads=[xr.b], writes=[h16.b])
                sst[s].update(xr=xr, h16=h16)

            def T1(s):
                r0, R = subs[s]
                xr, h16 = sst[s]["xr"], sst[s]["h16"]
                for kc in range(8):
                    P.add("pe", lambda e, kc=kc: e.transpose(
                        out=psS0.bf()[:, kc * 128:kc * 128 + R], in_=h16.bf()[:R, kc * 128:(kc + 1) * 128], identity=identb[:R, :R]),
                        reads=[h16.b, b_idb], writes=[psS0.b])
                tpool.release(h16)
                h1T = tpool.alloc()
                P.add("act", lambda e: e.activation(
                    out=h1T.bf().rearrange("p (a b) -> p a b", b=128)[:, :, :R],
                    in_=psS0.bf().rearrange("p (a b) -> p a b", b=128)[:, :, :R], func=AF.Copy),
                    reads=[psS0.b], writes=[h1T.b])
                sst[s]["h1T"] = h1T

            def T2(s):
                r0, R = subs[s]
                xr, h1T = sst[s]["xr"], sst[s]["h1T"]
                pspg = [psG.alloc(), psG.alloc()]
                for hf in range(2):
                    for kc in range(8):
                        P.add("pe", lambda e, ps=pspg[hf], kc=kc, hf=hf: e.matmul(
                            ps.t[:(128 if wide else R), :], lhsT=h1T.bf()[:, kc * 128:kc * 128 + (128 if wide else R)],
                            rhs=wpg[kc].t[:, hf * 512:(hf + 1) * 512],
                            start=(kc == 0), stop=(kc == 7)),
                            reads=[h1T.b, wpg[kc].b], writes=[pspg[hf].b])
                tpool.release(h1T)
                pspe = [psG.alloc(), psG.alloc()]
                for hf in range(2):
                    for j in range(2):
                        P.add("pe", lambda e, ps=pspe[hf], j=j, hf=hf: e.matmul(
                            ps.t[:R, :], lhsT=pT[:, j, s * 128:s * 128 + R], rhs=wpe[j].t[:, hf * 512:(hf + 1) * 512],
                            start=(j == 0), stop=(j == 1)),
                            reads=[b_pT, wpe[j].b], writes=[pspe[hf].b])
                if s == nsub - 1 and not hold:
                    release(wpg + wpe)
                for hf in range(2):
                    thp = tpool.alloc()
                    P.add("act", lambda e, thp=thp, ps=pspg[hf]: e.activation(out=thp.t[:R, :], in_=ps.t[:R, :], func=AF.Tanh, scale=0.5),
                          reads=[pspg[hf].b], writes=[thp.b])
                    P.add("dve", lambda e, thp=thp, ps=pspe[hf]: e.scalar_tensor_tensor(
                        out=thp.t[:R, :], in0=thp.t[:R, :], scalar=1.0, in1=ps.t[:R, :], op0=ALU.add, op1=ALU.mult),
                        reads=[thp.b, pspe[hf].b], writes=[thp.b])
                    P.add("dve", lambda e, thp=thp, hf=hf: e.scalar_tensor_tensor(
                        out=xr.t[:R, hf * 512:(hf + 1) * 512], in0=thp.t[:R, :], scalar=0.5, in1=xr.t[:R, hf * 512:(hf + 1) * 512],
                        op0=ALU.mult, op1=ALU.add),
                        reads=[thp.b, xr.b], writes=[xr.b])
                    tpool.release(thp)
                    psG.release(pspg[hf])
                    psG.release(pspe[hf])

            def SQ(s):
                r0, R = subs[s]
                xr = sst[s]["xr"]
                junk = tpool.alloc()
                sq2 = smalls[:, 16 + s:17 + s]
                P.add("act", lambda e: e.activation(out=junk.bf()[:R, :], in_=xr.t[:R, :], func=AF.Square, accum_out=sq2[:R, :]),
                      reads=[xr.b], writes=[junk.b, b_sq2s[0]])
                tpool.release(junk)

            def FIN():
                R0 = subs[0][1]
                sqa = smalls[:, 16:16 + nsub]
                P.add("act", lambda e: e.activation(out=sqa[:R0, :], in_=sqa[:R0, :], func=AF.Sqrt, scale=1.0 / D, bias=EPS),
                      reads=[b_sq2s[0]], writes=[b_sq2s[0]])
                P.add("dve", lambda e: e.reciprocal(out=sqa[:R0, :], in_=sqa[:R0, :]), reads=[b_sq2s[0]], writes=[b_sq2s[0]])
                for s in range(nsub):
                    r0, R = subs[s]
                    xr = sst[s]["xr"]
                    P.add("dve", lambda e, xr=xr, R=R, s=s: e.scalar_tensor_tensor(
                        out=xr.t[:R, :], in0=xr.t[:R, :], scalar=smalls[:R, 16 + s:17 + s], in1=gfb_sb[:R, :],
                        op0=ALU.mult, op1=ALU.mult),
                        reads=[xr.b, b_sq2s[0], b_gfb], writes=[xr.b])
                    P.add("sp", lambda e, xr=xr, R=R, r0=r0: e.dma_start(out=yd[r0:r0 + R, :], in_=xr.t[:R, :]),
                          reads=[xr.b], dma_key=xr.b)
                    rowpool.release(xr)

            order = []
            for s in range(nsub + 2):
                if s < nsub:
                    order.append((M, s))
                if 0 <= s - 2 < nsub:
                    order.append((T2, s - 2))
                if 0 <= s - 1 < nsub:
                    order.append((T1, s - 1))
                if 0 <= s - 2 < nsub:
                    order.append((SQ, s - 2))
            for (fn, s) in order:
                fn(s)
                yield fn.__name__ + str(s)
            FIN()
            yield "FIN"

    tiles = []
    for t in range(SEQ // TN):
        subs = [(t * TN + i * 128, 128) for i in range(TN // 128)]
        tiles.append(Tile(t, TN, subs, x_p, p_p, y_p, False, t == 0, t == SEQ // TN - 1))
    tiles.append(Tile(SEQ // TN, NS, [(0, NS)], x_s, p_s, y_s, True, False, False))
    tiles[0].pre()
    sample = tiles[-1]
    sample.nxt = None
    prompts = tiles[:-1]
    for i, T_ in enumerate(prompts):
        nxt = prompts[i + 1] if i + 1 < len(prompts) else None
        last = (nxt is None)
        T_.nxt = nxt
        if last:
            T_.comp = sample
        for _ in T_.ab_gen():
            pass
        if last:
            sample.sample_state()
        T_.a3(nxt)
        if last:
            sample.sample_state_trans()
        if nxt is not None:
            nxt.pre_trans()
        if i == len(prompts) - 2:
            sample.pre()
        if last:
            for _ in sample.ab_gen():
                pass
            sample.a3(None)
            xr0 = rowpool.alloc()
            P.add("sp", lambda e: e.dma_start(out=xr0.t[:NS, :], in_=x_s[0:NS, :]), writes=[xr0.b], dma_key=xr0.b)
            sample.xr_pre = xr0
            for _ in T_.post_gen(hold=True):
                pass
            for _ in sample.post_gen(pieces=T_.post_pieces):
                pass
        else:
            for _ in T_.post_gen():
                pass
    P.emit()
    return nc


_NC_CACHE = {}


def kernel(x_prompt, x_sample, p_prompt, p_sample, state_conv_a, state_conv_b, state_h,
           g_norm, w_in, w_dw_a, b_dw_a, ln_g, ln_b, w_conv_b, b_conv_b,
           w_r, b_r, w_i, b_i, lam, w_out, w_pe, w_pg, g_final):
    f = lambda a: np.ascontiguousarray(np.asarray(a, dtype=np.float32))
    x_prompt, x_sample, p_prompt, p_sample = f(x_prompt), f(x_sample), f(p_prompt), f(p_sample)
    state_conv_a, state_conv_b, state_h = f(state_conv_a), f(state_conv_b), f(state_h)
    w_in_l = f(f(w_in)[0].reshape(8, 128, 40, 128).transpose(2, 1, 0, 3).reshape(40, 128, 1024))
    w_out_l = f(f(w_out)[0].reshape(16, 128, 1024))
    w_pg_l = f(f(w_pg)[0].reshape(8, 128, 1024))
    w_pe_l = f(f(w_pe)[0].reshape(2, 128, 1024))
    w_r_l = f(f(w_r)[0].transpose(1, 0, 2).reshape(128, 1024))
    w_i_l = f(f(w_i)[0].transpose(1, 0, 2).reshape(128, 1024))
    fm = lambda v: f(v).reshape(8, 128).T
    vecs = f(np.stack([fm(g_norm), fm(b_dw_a), fm(ln_g), fm(ln_b), fm(b_conv_b), fm(b_r), fm(b_i), fm(lam)], axis=1)
             .reshape(128, 64))
    wdw = f(f(w_dw_a)[0].reshape(KA, 8, 128).transpose(2, 1, 0).reshape(128, 8 * KA))
    wcb = f(f(w_conv_b)[0].reshape(KB, 8, 128).transpose(2, 1, 0).reshape(128, 8 * KB))
    gfb = f(np.broadcast_to(f(g_final).reshape(1, D), (128, D)))
    ident = np.eye(128, dtype=np.float32)

    if "nc" not in _NC_CACHE:
        _NC_CACHE["nc"] = build_nc()
    nc = _NC_CACHE["nc"]
    in_maps = []
    for i in range(NCORES):
        sl = slice(i * NS, (i + 1) * NS)
        in_maps.append(dict(
            x_p=x_prompt[i], p_p=p_prompt[0, i], x_s=f(x_sample[sl, 0]), p_s=f(p_sample[0, sl, 0]),
            sca=f(state_conv_a[0, sl].reshape(NS * 30, D)), scb=f(state_conv_b[0, sl].reshape(NS * 3, D)),
            sh=f(state_h[0, sl]),
            w_in=w_in_l, w_out=w_out_l, w_pg=w_pg_l, w_pe=w_pe_l, w_r=w_r_l, w_i=w_i_l,
            vecs=vecs, wdw=wdw, wcb=wcb, gfb=gfb, ident=ident))
    res = run_bass_kernel_spmd(nc, in_maps, core_ids=list(range(NCORES)))
    r = res.results
    y_prompt = np.stack([r[i]["y_p"] for i in range(NCORES)], 0)
    y_sample = np.concatenate([r[i]["y_s"] for i in range(NCORES)], 0)[:, None, :]
    na_p = np.stack([r[i]["na_p"] for i in range(NCORES)], 0)[None]
    nb_p = np.stack([r[i]["nb_p"] for i in range(NCORES)], 0)[None]
    nh_p = np.concatenate([r[i]["nh_p"] for i in range(NCORES)], 0)[None]
    na_s = np.concatenate([r[i]["na_s"] for i in range(NCORES)], 0)[None]
    nb_s = np.concatenate([r[i]["nb_s"] for i in range(NCORES)], 0)[None]
    nh_s = np.concatenate([r[i]["nh_s"] for i in range(NCORES)], 0)[None]
    out = (y_prompt, y_sample, na_p, nb_p, nh_p, na_s, nb_s, nh_s)
    return tuple(np.ascontiguousarray(o.astype(np.float32)) for o in out)
```

```python
from contextlib import ExitStack

import numpy as np
import concourse.bass as bass
import concourse.mybir as mybir
from concourse.bass_utils import run_bass_kernel_spmd

F32 = mybir.dt.float32
BF16 = mybir.dt.bfloat16
AF = mybir.ActivationFunctionType
ALU = mybir.AluOpType

NCORES = 8
D = 1024
SEQ = 2048
NS = 16
KA = 31
KB = 4
EPS = 1e-6
TN = 512
RING = 28
NTMP = 11
NHALF = 8
LOOKAHEAD = 24
ZIPK = 5

ENGINES = ("pe", "act", "dve", "pool", "sp")


class Buf:
    __slots__ = ("name", "last_w", "readers", "sem", "dma_count")

    def __init__(self, name):
        self.name = name
        self.last_w = None
        self.readers = []
        self.sem = None
        self.dma_count = 0


class Op:
    __slots__ = ("eng", "fn", "reads", "writes", "deps", "is_dma", "key", "flag", "semval", "idx")

    def __init__(self, eng, fn, reads, writes, is_dma, key):
        self.eng = eng
        self.fn = fn
        self.reads = reads
        self.writes = writes
        self.is_dma = is_dma
        self.key = key
        self.deps = []
        self.flag = False
        self.semval = 0


class Prog:
    def __init__(self, nc):
        self.nc = nc
        self.ops = []
        self.dma_keys = []

    def add(self, eng, fn, reads=(), writes=(), dma_key=None):
        is_dma = dma_key is not None
        op = Op(eng, fn, list(reads), list(writes), is_dma, dma_key)
        op.idx = len(self.ops)
        deps = {}
        for b in op.reads:
            w = b.last_w
            if w is not None:
                deps[w.idx] = (w, True)
        for b in op.writes:
            w = b.last_w
            if w is not None and w.idx not in deps:
                deps[w.idx] = (w, False)
            for r in b.readers:
                if r.idx not in deps:
                    deps[r.idx] = (r, False)
        for b in op.reads:
            b.readers.append(op)
        for b in op.writes:
            b.last_w = op
            b.readers = []
        for (d, raw) in deps.values():
            if d is op:
                continue
            if d.is_dma or op.is_dma or d.eng != op.eng:
                op.deps.append(d)
            elif op.eng != "pe":
                op.deps.append(d)
        if is_dma and dma_key not in self.dma_keys:
            self.dma_keys.append(dma_key)
        self.ops.append(op)
        return op

    def emit(self):
        nc = self.nc
        for op in self.ops:
            for d in op.deps:
                d.flag = True
        cnt = {e: 0 for e in ENGINES}
        for op in self.ops:
            if op.is_dma:
                op.key.dma_count += 16
                op.semval = op.key.dma_count
            elif op.flag:
                cnt[op.eng] += 1
                op.semval = cnt[op.eng]
        with ExitStack() as es:
            esem = {e: es.enter_context(nc.semaphore("s_" + e)) for e in ENGINES if e != "sp"}
            for k in self.dma_keys:
                k.sem = es.enter_context(nc.semaphore("d_" + k.name))
            block = es.enter_context(nc.Block())
            per_eng = {e: [o for o in self.ops if o.eng == e] for e in ENGINES}
            final_dma = [(k.sem, k.dma_count) for k in self.dma_keys]

            def run(engname, eng):
                waited = {}
                for op in per_eng[engname]:
                    need = {}
                    for d in op.deps:
                        s = d.key.sem if d.is_dma else esem[d.eng]
                        if need.get(s, 0) < d.semval:
                            need[s] = d.semval
                    for s, v in need.items():
                        if waited.get(s, 0) < v:
                            eng.wait_ge(s, v)
                            waited[s] = v
                    ins = op.fn(eng)
                    if op.is_dma:
                        ins.then_inc(op.key.sem, 16)
                    elif op.flag:
                        ins.then_inc(esem[op.eng], 1)
                if engname == "sp":
                    for s, v in final_dma:
                        if waited.get(s, 0) < v:
                            eng.wait_ge(s, v)

            @block.tensor
            def _(e):
                run("pe", e)

            @block.scalar
            def _(e):
                run("act", e)

            @block.vector
            def _(e):
                run("dve", e)

            @block.gpsimd
            def _(e):
                run("pool", e)

            @block.sync
            def _(e):
                run("sp", e)


class Slot:
    def __init__(self, t, buf, nbytes):
        self.t = t
        self.b = buf
        self.nbytes = nbytes

    def f32(self):
        return self.t[:]

    def bf(self):
        return self.t[:].bitcast(BF16)


class Pool_:
    def __init__(self, slots):
        self.slots = slots
        self.i = 0

    def next(self):
        s = self.slots[self.i % len(self.slots)]
        self.i += 1
        return s


def build_nc():
    nc = bass.Bass("TRN2", target_bir_lowering=False)

    def din(name, shape):
        return nc.dram_tensor(name, shape, F32, kind="ExternalInput").ap()

    def dout(name, shape):
        return nc.dram_tensor(name, shape, F32, kind="ExternalOutput").ap()

    x_p = din("x_p", [SEQ, D])
    p_p = din("p_p", [SEQ, 256])
    x_s = din("x_s", [NS, D])
    p_s = din("p_s", [NS, 256])
    sca = din("sca", [NS * 30, D])
    scb = din("scb", [NS * 3, D])
    sh = din("sh", [NS, D])
    w_in = din("w_in", [40, 128, 1024])
    w_out = din("w_out", [16, 128, 1024])
    w_pg = din("w_pg", [8, 128, 1024])
    w_pe = din("w_pe", [2, 128, 1024])
    w_r = din("w_r", [128, 1024])
    w_i = din("w_i", [128, 1024])
    vecs = din("vecs", [128, 64])
    wdw = din("wdw", [128, 8 * KA])
    wcb = din("wcb", [128, 8 * KB])
    gfb = din("gfb", [128, D])
    ident = din("ident", [128, 128])

    y_p = dout("y_p", [SEQ, D])
    y_s = dout("y_s", [NS, D])
    na_p = dout("na_p", [30, D])
    nb_p = dout("nb_p", [3, D])
    nh_p = dout("nh_p", [1, D])
    na_s = dout("na_s", [NS, 30, D])
    nb_s = dout("nb_s", [NS, 3, D])
    nh_s = dout("nh_s", [NS, D])

    wscr = nc.dram_tensor("wscr", [66, 128, 1024], BF16).ap()
    dscr = nc.dram_tensor("dscr", [8, 128, KA * 128], BF16).ap()
    b_wscr = [Buf(f"wscr{j}") for j in range(66)]
    b_dscr = [Buf(f"dscr{j}") for j in range(8)]

    P = Prog(nc)
    uid = [0]

    def sb(shape, dt, name=None):
        uid[0] += 1
        return nc.alloc_sbuf_tensor(name or f"t{uid[0]}", shape, dt)

    def mkslots(n, prefix):
        return [Slot(sb([128, 512], F32, f"{prefix}{i}"), Buf(f"{prefix}{i}"), 2048) for i in range(n)]

    class FreePool:
        def __init__(self, slots, name):
            self.free = list(slots)
            self.name = name

        def alloc(self):
            assert self.free, f"pool {self.name} exhausted"
            return self.free.pop(0)

        def release(self, s):
            assert s not in self.free
            self.free.append(s)

    ring = [Slot(sb([128, 1024], BF16, f"ring{i}"), Buf(f"ring{i}"), 2048) for i in range(RING)]
    tpool = FreePool(mkslots(NTMP, "tmp"), "tmp")
    hpool = FreePool([Slot(sb([128, 512], BF16, f"hb{i}"), Buf(f"hb{i}"), 1024) for i in range(NHALF)], "half")
    rowpool = FreePool([Slot(sb([128, 1024], F32, f"row{i}"), Buf(f"row{i}"), 4096) for i in range(5)], "row")
    psG = FreePool([Slot(nc.alloc_psum_tensor(f"psg{i}", [128, 512], F32), Buf(f"psg{i}"), 2048) for i in range(6)], "psum")
    psS0 = Slot(nc.alloc_psum_tensor("pss0", [128, 512], F32), Buf("pss0"), 2048)
    psS1 = Slot(nc.alloc_psum_tensor("pss1", [128, 512], F32), Buf("pss1"), 2048)
    pinpool = FreePool([Slot(sb([128, 256], F32, f"pin{i}"), Buf(f"pin{i}"), 1024) for i in range(2)], "pin")

    cv = sb([128, 8, TN], F32, "cv")
    b_cv = [Buf(f"cv{c}") for c in range(8)]
    mixT = sb([128, 16, TN], BF16, "mixT")
    b_mix = [Buf(f"mix{c}") for c in range(16)]
    uT = sb([128, 8, TN], BF16, "uT")
    b_uT = Buf("uT")
    pTs = [sb([128, 2, TN], BF16, f"pT{i}") for i in range(2)]
    b_pTs = [Buf("pT0"), Buf("pT1")]
    uT_s = sb([128, 8, NS], BF16, "uT_s")
    b_uT_s = Buf("uT_s")
    mixT_s = sb([128, 16, NS], BF16, "mixT_s")
    b_mix_s = [Buf(f"mixs{c}") for c in range(16)]
    pT_s = sb([128, 2, NS], BF16, "pT_s")
    b_pT_s = Buf("pT_s")
    z_s = sb([128, 40, NS], F32, "z_s")
    b_zs = [Buf(f"zs{j}") for j in range(40)]
    vbuf = sb([128, 8, 30 + TN], BF16, "vbuf")
    b_vb = [Buf(f"vb{c}") for c in range(8)]
    bxbuf = sb([128, 8, 3 + TN], BF16, "bxbuf")
    b_bb = [Buf(f"bb{c}") for c in range(8)]
    diagA = [sb([128, KA, 128], BF16, f"diagA{i}") for i in range(2)]
    b_dA = [Buf("dA0"), Buf("dA1")]
    b_dAh = [Buf("dA0h"), Buf("dA1h")]
    KH = 16
    diagB = [sb([128, KB, 128], BF16, f"diagB{i}") for i in range(2)]
    b_dB = [Buf("dB0"), Buf("dB1")]
    wr_sb = sb([128, 1024], BF16, "wr_sb")
    wi_sb = sb([128, 1024], BF16, "wi_sb")
    b_wr = Buf("wr")
    b_wi = Buf("wi")
    vecs_sb = sb([128, 8, 8], F32, "vecs_sb")
    b_vecs = Buf("vecs")
    der = sb([128, 10, 8], F32, "der")
    b_der = Buf("der")
    wdw_sb = sb([128, 8, KA], F32, "wdw_sb")
    b_wdw = Buf("wdw")
    hwdw = sb([128, 8, KA], F32, "hwdw")
    b_hwdw = Buf("hwdw")
    wcb_sb = sb([128, 8, KB], F32, "wcb_sb")
    b_wcb = Buf("wcb")
    gfb_sb = sb([128, D], F32, "gfb_sb")
    b_gfb = Buf("gfb")
    identf = sb([128, 128], F32, "identf")
    b_idf = Buf("identf")
    identb = sb([128, 128], BF16, "identb")
    b_idb = Buf("identb")
    onesb = sb([128, 128], BF16, "onesb")
    b_ones = Buf("onesb")
    mean_sb = sb([128, TN], F32, "mean_sb")
    b_mean = Buf("mean")
    rstd_ln = sb([128, TN], F32, "rstd_ln")
    b_rln = Buf("rstdln")
    hstate = sb([128, 8], F32, "hstate")
    b_hsts = [Buf(f"hstate{c}") for c in range(8)]
    shT = sb([128, 8, NS], F32, "shT")
    b_shT = Buf("shT")
    stA = sb([128, 8, 30], F32, "stA")
    stB = sb([128, 8, NS], F32, "stB")
    stH = sb([128, 8, NS], F32, "stH")
    b_stA, b_stB, b_stH = Buf("stA"), Buf("stB"), Buf("stH")
    smalls = sb([128, 32], F32, "smalls")
    b_nbs = Buf("nbs")
    b_ssq = Buf("ssq")
    b_rstd = Buf("rstd")
    b_sq2s = [Buf(f"sq2_{i}") for i in range(4)]

    V_GN, V_BDW, V_LNG, V_LNB, V_BCB, V_BR, V_BI, V_LAM = range(8)
    D_HG, D_HB, D_HBR, D_HBI, D_S1, D_S2, D_T0, D_T1, D_T2, D_T3 = range(10)

    def vcol(i, c):
        return vecs_sb[:, i, c:c + 1]

    def dcol(i, c):
        return der[:, i, c:c + 1]

    def dma_in(dst_ap, src_ap, buf, eng="sp"):
        P.add(eng, lambda e: e.dma_start(out=dst_ap, in_=src_ap), writes=[buf], dma_key=buf)

    dma_in(vecs_sb[:].rearrange("p a b -> p (a b)"), vecs, b_vecs)
    dma_in(wdw_sb[:].rearrange("p a b -> p (a b)"), wdw, b_wdw)
    dma_in(wcb_sb[:].rearrange("p a b -> p (a b)"), wcb, b_wcb)
    dma_in(gfb_sb[:], gfb, b_gfb)
    dma_in(identf[:], ident, b_idf)
    dma_in(wr_sb[:], w_r, b_wr, eng="pool")
    dma_in(wi_sb[:], w_i, b_wi, eng="pool")

    P.add("dve", lambda e: e.tensor_copy(out=identb[:], in_=identf[:]), reads=[b_idf], writes=[b_idb])
    P.add("dve", lambda e: e.memset(onesb[:], 1.0 / 1024.0), writes=[b_ones])
    P.add("dve", lambda e: e.tensor_scalar(out=hwdw[:], in0=wdw_sb[:], scalar1=0.5, scalar2=None, op0=ALU.mult),
          reads=[b_wdw], writes=[b_hwdw])
    P.add("dve", lambda e: e.tensor_scalar(out=der[:, D_HG:D_HB + 1, :], in0=vecs_sb[:, V_LNG:V_LNB + 1, :], scalar1=0.5,
                                           scalar2=None, op0=ALU.mult), reads=[b_vecs], writes=[b_der])
    P.add("dve", lambda e: e.tensor_scalar(out=der[:, D_HBR:D_HBI + 1, :], in0=vecs_sb[:, V_BR:V_BI + 1, :], scalar1=0.5,
                                           scalar2=None, op0=ALU.mult), reads=[b_vecs], writes=[b_der])
    P.add("act", lambda e: e.activation(out=der[:, D_T0, :], in_=vecs_sb[:, V_LAM, :], func=AF.Abs),
          reads=[b_vecs], writes=[b_der])
    P.add("act", lambda e: e.activation(out=der[:, D_T1, :], in_=der[:, D_T0, :], func=AF.Exp, scale=-1.0),
          reads=[b_der], writes=[b_der])
    P.add("act", lambda e: e.activation(out=der[:, D_T2, :], in_=der[:, D_T1, :], func=AF.Ln, bias=1.0),
          reads=[b_der], writes=[b_der])
    P.add("dve", lambda e: e.tensor_scalar(out=der[:, D_T3, :], in0=vecs_sb[:, V_LAM, :], scalar1=-1.0, scalar2=0.0,
                                           op0=ALU.mult, op1=ALU.max), reads=[b_vecs], writes=[b_der])
    P.add("dve", lambda e: e.tensor_tensor(out=der[:, D_T3, :], in0=der[:, D_T3, :], in1=der[:, D_T2, :], op=ALU.add),
          reads=[b_der], writes=[b_der])
    P.add("dve", lambda e: e.tensor_scalar(out=der[:, D_S1, :], in0=der[:, D_T3, :], scalar1=-4.0, scalar2=None,
                                           op0=ALU.mult), reads=[b_der], writes=[b_der])
    P.add("dve", lambda e: e.tensor_scalar(out=der[:, D_S2, :], in0=der[:, D_T3, :], scalar1=-8.0, scalar2=None,
                                           op0=ALU.mult), reads=[b_der], writes=[b_der])

    def ab_schedule(sample=False):
        sched = []
        if sample:
            for nm in ("S0", "A1", "S1", "A2", "S2a", "S3", "S2b", "S4"):
                for c in range(8):
                    sched.append((nm, c))
            return sched
        for j in range(8 + 2):
            if j == 8:
                sched.append(("PRESQ", 0))
            if j == 9:
                sched.append(("PREEL", 0))
            if j == 0:
                sched.append(("S0", 0))
            if j + 1 < 8:
                sched.append(("S0", j + 1))
            if 0 <= j - 2 < 8:
                sched.append(("A2", j - 2))
            if j < 8:
                sched.append(("S1", j))
            if 0 <= j - 1 < 8:
                sched.append(("A1", j - 1))
            if j < 8:
                sched.append(("S2a", j))
                sched.append(("S2b", j))
                sched.append(("S3", j))
                sched.append(("S4", j))
        return sched

    def mk_pieces(sample):
        tp_ = []
        for (st, c) in ab_schedule(sample):
            if st == "A1":
                tp_ += [c, 8 + c]
            elif st == "S0":
                tp_ += [24 + c]
            elif st == "S3":
                tp_ += [32 + c]
        tp_ += [16 + c for c in range(8)]
        tp_ += [40 + k for k in range(16)]
        tp_ += [56 + k for k in range(8)]
        tp_ += [64 + k for k in range(2)]
        assert sorted(tp_) == list(range(66))
        return tp_

    def piece_src(j):
        if j < 40:
            return w_in[j]
        if j < 56:
            return w_out[j - 40]
        if j < 64:
            return w_pg[j - 56]
        return w_pe[j - 64]

    all_pieces = mk_pieces(False) * (SEQ // TN)
    rstate = {"issued": 0, "used": 0, "released": 0}

    def ring_pump(upto):
        while rstate["issued"] < len(all_pieces) and rstate["issued"] <= upto \
                and rstate["issued"] < rstate["released"] + RING:
            m = rstate["issued"]
            sl = ring[m % RING]
            j = all_pieces[m]
            if m < 66:
                src = piece_src(j)
                P.add("pool", lambda e, sl=sl, src=src: e.dma_start(out=sl.t[:], in_=src), writes=[sl.b],
                      dma_key=b_rst[m % RING])
                P.add("sp", lambda e, sl=sl, j=j: e.dma_start(out=wscr[j], in_=sl.t[:]), reads=[sl.b],
                      writes=[b_wscr[j]], dma_key=sl.b)
            else:
                P.add("sp", lambda e, sl=sl, j=j: e.dma_start(out=sl.t[:], in_=wscr[j]), reads=[b_wscr[j]],
                      writes=[sl.b], dma_key=sl.b)
            rstate["issued"] += 1

    rflags = [False] * len(all_pieces)
    b_rst = [Buf(f"rst{i}") for i in range(RING)]

    def use_piece(j_expect):
        n = rstate["used"]
        assert all_pieces[n] == j_expect, (n, all_pieces[n], j_expect)
        rstate["used"] += 1
        ring_pump(n + LOOKAHEAD)
        assert rstate["issued"] > n, "ring deadlock"
        sl = ring[n % RING]
        sl.cur = n
        return sl

    def release(slots):
        for sl in slots:
            rflags[sl.cur] = True
        while rstate["released"] < len(all_pieces) and rflags[rstate["released"]]:
            rstate["released"] += 1
        ring_pump(rstate["used"] - 1 + LOOKAHEAD)

    def state_out(src, b_src, W, scale, dst_ap):
        pa, pb = psG.alloc(), psG.alloc()
        for c in range(8):
            ps = pa if c < 4 else pb
            col = (c % 4) * 128
            P.add("pe", lambda e, ps=ps, col=col, c=c: e.transpose(out=ps.t[:W, col:col + 128], in_=src[:, c, :W],
                                                                    identity=identf[:]),
                  reads=[b_src, b_idf], writes=[ps.b])
        row = rowpool.alloc()
        for hf, ps in enumerate((pa, pb)):
            P.add("act", lambda e, ps=ps, hf=hf, row=row: e.activation(out=row.t[:W, hf * 512:(hf + 1) * 512],
                                                                         in_=ps.t[:W, :], func=AF.Copy, scale=scale),
                  reads=[ps.b], writes=[row.b])
        P.add("sp", lambda e, row=row: e.dma_start(out=dst_ap, in_=row.t[:W, :]), reads=[row.b], dma_key=row.b)
        psG.release(pa)
        psG.release(pb)
        rowpool.release(row)

    class Tile:
        def __init__(self, idx, N, subs, xd, pd, yd, is_sample, is_first_prompt, is_last_prompt):
            self.idx = idx
            self.N = N
            self.subs = subs
            self.xd, self.pd, self.yd = xd, pd, yd
            self.is_sample = is_sample
            self.is_first = is_first_prompt
            self.is_last = is_last_prompt
            self.stride = NS if is_sample else 1
            self.pT = pT_s if is_sample else pTs[idx % 2]
            self.b_pT = b_pT_s if is_sample else b_pTs[idx % 2]
            self.uT = uT_s if is_sample else uT
            self.b_uT = b_uT_s if is_sample else b_uT
            self.mixT = mixT_s if is_sample else mixT
            self.b_mix = b_mix_s if is_sample else b_mix
            self.comp = None

        def pre(self):
            self.pre_sq()
            self.pre_elem()
            self.pre_trans()

        def pre_sq(self):
            N, subs, xd, pd = self.N, self.subs, self.xd, self.pd
            pT, b_pT = self.pT, self.b_pT
            ssq = smalls[:, 0:4]
            rstd = smalls[:, 8:12]
            xins = []
            for s, (r0, R) in enumerate(subs):
                xin = rowpool.alloc()
                xins.append(xin)
                P.add("sp", lambda e, xin=xin, r0=r0, R=R: e.dma_start(out=xin.t[:R, :], in_=xd[r0:r0 + R, :]),
                      writes=[xin.b], dma_key=xin.b)
                junk = tpool.alloc()
                P.add("act", lambda e, xin=xin, junk=junk, R=R, s=s: e.activation(
                    out=junk.bf()[:R, :], in_=xin.t[:R, :], func=AF.Square, accum_out=ssq[:R, s:s + 1]),
                    reads=[xin.b], writes=[junk.b, b_ssq])
                tpool.release(junk)
            R0 = subs[0][1]
            ns = len(subs)
            P.add("act", lambda e: e.activation(out=ssq[:R0, 0:ns], in_=ssq[:R0, 0:ns], func=AF.Sqrt, scale=1.0 / D, bias=EPS),
                  reads=[b_ssq], writes=[b_ssq])
            P.add("dve", lambda e: e.reciprocal(out=rstd[:R0, 0:ns], in_=ssq[:R0, 0:ns]), reads=[b_ssq], writes=[b_rstd])
            self.xins = xins

        def pre_elem(self):
            N, subs, xd, pd = self.N, self.subs, self.xd, self.pd
            pT, b_pT = self.pT, self.b_pT
            rstd = smalls[:, 8:12]
            xins = self.xins
            xn16s = []
            for s, (r0, R) in enumerate(subs):
                xn = tpool.alloc()
                xn16s.append(xn)
                xin = xins[s]
                P.add("act", lambda e, xn=xn, xin=xin, R=R, s=s: e.activation(
                    out=xn.bf()[:R, :], in_=xin.t[:R, :], func=AF.Copy, scale=rstd[:R, s:s + 1]),
                    reads=[xin.b, b_rstd], writes=[xn.b])
                rowpool.release(xin)
            self.xn16s = xn16s
            p16s = []
            for s, (r0, R) in enumerate(subs):
                pin = pinpool.alloc()
                P.add("sp", lambda e, pin=pin, r0=r0, R=R: e.dma_start(out=pin.t[:R, 0:256], in_=pd[r0:r0 + R, :]),
                      writes=[pin.b], dma_key=pin.b)
                p16 = hpool.alloc()
                p16s.append(p16)
                P.add("dve", lambda e, pin=pin, p16=p16, R=R: e.tensor_copy(out=p16.t[:R, 0:256], in_=pin.t[:R, 0:256]),
                      reads=[pin.b], writes=[p16.b])
                pinpool.release(pin)
            self.p16s = p16s

        def pre_trans(self):
            N, subs = self.N, self.subs
            pT, b_pT = self.pT, self.b_pT
            xn16s = self.xn16s
            p16s = self.p16s
            for j in range(2):
                pt = (psS0, psS1)[j]
                for s, (r0, R) in enumerate(subs):
                    p16 = p16s[s]
                    P.add("pe", lambda e, pt=pt, p16=p16, R=R, s=s, j=j: e.transpose(
                        out=pt.bf()[:, s * 128:s * 128 + R], in_=p16.t[:R, j * 128:(j + 1) * 128], identity=identb[:R, :R]),
                        reads=[p16.b, b_idb], writes=[pt.b])
                P.add("act", lambda e, pt=pt, j=j: e.activation(out=pT[:, j, :N], in_=pt.bf()[:, :N], func=AF.Copy),
                      reads=[pt.b], writes=[b_pT])
            for p16 in p16s:
                hpool.release(p16)
            for kc in range(8):
                pt = psG.alloc()
                for s, (r0, R) in enumerate(subs):
                    xn = xn16s[s]
                    P.add("pe", lambda e, pt=pt, xn=xn, R=R, s=s, kc=kc: e.transpose(
                        out=pt.bf()[:, s * 128:s * 128 + R], in_=xn.bf()[:R, kc * 128:(kc + 1) * 128], identity=identb[:R, :R]),
                        reads=[xn.b, b_idb], writes=[pt.b])
                P.add("dve", lambda e, pt=pt, kc=kc: e.tensor_scalar(
                    out=self.uT[:, kc, :N], in0=pt.bf()[:, :N], scalar1=vcol(V_GN, kc), scalar2=None, op0=ALU.mult),
                    reads=[pt.b, b_vecs], writes=[self.b_uT])
                psG.release(pt)
            for xn in xn16s:
                tpool.release(xn)

            if self.is_first:
                for c in range(8):
                    P.add("pool", lambda e, c=c: e.memset(vbuf[:, c, 0:30], 0.0), writes=[b_vb[c]])
                    P.add("pool", lambda e, c=c: e.memset(bxbuf[:, c, 0:3], 0.0), writes=[b_bb[c]])
                P.add("pool", lambda e: e.memset(hstate[:], 0.0), writes=b_hsts)

        def sample_state(self):
            grows = [rowpool.alloc() for _ in range(4)]
            for g in range(4):
                P.add("sp", lambda e, g=g: e.dma_start(out=grows[g].t[:120, :], in_=sca[g * 120:(g + 1) * 120, :]),
                      writes=[grows[g].b], dma_key=grows[g].b)
            P.add("sp", lambda e: e.dma_start(out=na_s[:, 0:29, :], in_=sca.rearrange("(s k) d -> s k d", k=30)[:, 1:30, :]),
                  dma_key=b_nbs)
            P.add("sp", lambda e: e.dma_start(out=nb_s[:, 0:2, :], in_=scb.rearrange("(s k) d -> s k d", k=3)[:, 1:3, :]),
                  dma_key=b_nbs)
            self.grows = grows

        def sample_state_trans(self):
            grows = self.grows
            for c in range(8):
                ps = psG.alloc()
                for g in range(4):
                    P.add("pe", lambda e, ps=ps, g=g, c=c: e.transpose(
                        out=ps.t[:, g * 120:(g + 1) * 120], in_=grows[g].t[:120, c * 128:(c + 1) * 128],
                        identity=identf[:120, :120]),
                        reads=[grows[g].b, b_idf], writes=[ps.b])
                P.add("act", lambda e, ps=ps, c=c: e.activation(
                    out=vbuf[:, c, 0:480].rearrange("p (k s) -> p k s", s=NS),
                    in_=ps.t[:, 0:480].rearrange("p (s k) -> p k s", k=30), func=AF.Copy, scale=2.0),
                    reads=[ps.b], writes=[b_vb[c]])
                psG.release(ps)
            for g in range(4):
                rowpool.release(grows[g])
            rowb = rowpool.alloc()
            P.add("sp", lambda e: e.dma_start(out=rowb.t[:48, :], in_=scb[:, :]), writes=[rowb.b], dma_key=rowb.b)
            for c in range(8):
                ps = psG.alloc()
                P.add("pe", lambda e, ps=ps, c=c: e.transpose(out=ps.t[:, 0:48], in_=rowb.t[:48, c * 128:(c + 1) * 128],
                                                             identity=identf[:48, :48]),
                      reads=[rowb.b, b_idf], writes=[ps.b])
                P.add("act", lambda e, ps=ps, c=c: e.activation(
                    out=bxbuf[:, c, 0:48].rearrange("p (k s) -> p k s", s=NS),
                    in_=ps.t[:, 0:48].rearrange("p (s k) -> p k s", k=3), func=AF.Copy),
                    reads=[ps.b], writes=[b_bb[c]])
                psG.release(ps)
            rowpool.release(rowb)
            rowh = rowpool.alloc()
            P.add("sp", lambda e: e.dma_start(out=rowh.t[:NS, :], in_=sh[:, :]), writes=[rowh.b], dma_key=rowh.b)
            ps = psG.alloc()
            for c in range(8):
                P.add("pe", lambda e, ps=ps, c=c: e.transpose(out=ps.t[:, c * NS:(c + 1) * NS],
                                                             in_=rowh.t[:NS, c * 128:(c + 1) * 128],
                                                             identity=identf[:NS, :NS]),
                      reads=[rowh.b, b_idf], writes=[ps.b])
            P.add("act", lambda e, ps=ps: e.activation(out=shT[:].rearrange("p a b -> p (a b)"), in_=ps.t[:, 0:8 * NS],
                                                       func=AF.Copy), reads=[ps.b], writes=[b_shT])
            psG.release(ps)
            rowpool.release(rowh)

        def inproj(self, ps, wsl, j):
            N = self.N
            for kc in range(8):
                P.add("pe", lambda e, kc=kc: e.matmul(ps.t[:, :N], lhsT=wsl.t[:, kc * 128:(kc + 1) * 128],
                                                      rhs=self.uT[:, kc, :N], start=(kc == 0), stop=(kc == 7)),
                      reads=[wsl.b, self.b_uT], writes=[ps.b])
            if self.comp is not None:
                pz = psG.alloc()
                for kc in range(8):
                    P.add("pe", lambda e, kc=kc: e.matmul(pz.t[:, :NS], lhsT=wsl.t[:, kc * 128:(kc + 1) * 128],
                                                          rhs=uT_s[:, kc, :], start=(kc == 0), stop=(kc == 7)),
                          reads=[wsl.b, b_uT_s], writes=[pz.b])
                P.add("dve", lambda e: e.tensor_copy(out=z_s[:, j, :], in_=pz.t[:, :NS]), reads=[pz.b], writes=[b_zs[j]])
                psG.release(pz)
            release([wsl])

        def zget(self, j):
            if self.is_sample:
                return Slot(z_s[:, j, :], b_zs[j], 64), False
            w = use_piece(j)
            ps = psG.alloc()
            self.inproj(ps, w, j)
            return ps, True

        def zput(self, ps, owned):
            if owned:
                psG.release(ps)

        def diag_load(self, c):
            P.add("sp", lambda e, c=c: e.dma_start(out=diagA[c % 2][:].rearrange("p a b -> p (a b)"), in_=dscr[c]),
                  reads=[b_dscr[c]], writes=[b_dA[c % 2], b_dAh[c % 2]], dma_key=b_dA[c % 2])

        def diag_build(self, c):
            dA = diagA[c % 2]
            P.add("dve", lambda e: e.tensor_tensor(
                out=dA[:, 0:KH, :], in0=identb[:].unsqueeze(1).to_broadcast([128, KH, 128]),
                in1=hwdw[:, c, 0:KH].unsqueeze(2).to_broadcast([128, KH, 128]), op=ALU.mult),
                reads=[b_idb, b_hwdw], writes=[b_dA[c % 2]])
            P.add("dve", lambda e: e.tensor_tensor(
                out=dA[:, KH:KA, :], in0=identb[:].unsqueeze(1).to_broadcast([128, KA - KH, 128]),
                in1=hwdw[:, c, KH:KA].unsqueeze(2).to_broadcast([128, KA - KH, 128]), op=ALU.mult),
                reads=[b_idb, b_hwdw], writes=[b_dAh[c % 2]])
            P.add("sp", lambda e: e.dma_start(out=dscr[c], in_=dA[:].rearrange("p a b -> p (a b)")),
                  reads=[b_dA[c % 2], b_dAh[c % 2]], writes=[b_dscr[c]], dma_key=b_dA[c % 2])

        def ab_gen(self):
            N, stride = self.N, self.stride
            is_sample, is_last = self.is_sample, self.is_last
            st = [dict() for _ in range(8)]
            self.pending = None
            if is_sample:
                bigs = [tpool.alloc(), tpool.alloc()]
                bigh = hpool.alloc()
                self.bigh = bigh
                def carve(big, width, nm):
                    out = []
                    for k in range(512 // width):
                        b = Buf(f"{nm}_{k}")
                        b.last_w = big.b.last_w
                        b.readers = list(big.b.readers)
                        out.append(Slot(big.t[:, k * width:(k + 1) * width], b, 64))
                    return out

                self.small = [(b_, carve(b_, NS, f"sm{bi}")) for bi, b_ in enumerate(bigs)]
                self.smallh = (bigh, carve(bigh, 2 * NS, "smh"))
                tp = FreePool([x for (_, lst) in self.small for x in lst], "smalltmp")
                hp = FreePool(list(self.smallh[1]), "smallhalf")
            else:
                tp, hp = tpool, hpool
            builds = (self.idx == 0)
            if is_sample:
                pass
            elif not builds:
                self.diag_load(0)
                self.diag_load(1)
            else:
                self.diag_build(0)
                self.diag_build(1)

            def A1(c):
                ps_av, o_av = self.zget(c)
                ps_ag, o_ag = self.zget(8 + c)
                th = tp.alloc()
                P.add("act", lambda e: e.activation(out=th.t[:, :N], in_=ps_ag.t[:, :N], func=AF.Tanh, scale=0.5),
                      reads=[ps_ag.b], writes=[th.b])
                o0 = 30 * stride
                P.add("dve", lambda e: e.scalar_tensor_tensor(
                    out=vbuf[:, c, o0:o0 + N], in0=th.t[:, :N], scalar=1.0, in1=ps_av.t[:, :N], op0=ALU.add, op1=ALU.mult),
                    reads=[th.b, ps_av.b], writes=[b_vb[c]])
                if is_sample or is_last:
                    W = NS if is_sample else 30
                    P.add("dve", lambda e: e.scalar_tensor_tensor(
                        out=stA[:, c, :W], in0=th.t[:, N - W:N], scalar=1.0, in1=ps_av.t[:, N - W:N], op0=ALU.add, op1=ALU.mult),
                        reads=[th.b, ps_av.b], writes=[b_stA])
                tp.release(th)
                self.zput(ps_av, o_av)
                self.zput(ps_ag, o_ag)

            def stats_mm(c, cv16, sq16):
                if is_sample:
                    P.add("pe", lambda e: e.matmul(psS1.t[:, :2 * N], lhsT=onesb[:], rhs=cv16.t[:, :2 * N], start=(c == 0), stop=(c == 7)),
                          reads=[cv16.b, b_ones], writes=[psS1.b])
                    hp.release(cv16)
                    return
                P.add("pe", lambda e: e.matmul(psS0.t[:, :N], lhsT=onesb[:], rhs=cv16.t[:, :N], start=(c == 0), stop=(c == 7)),
                      reads=[cv16.b, b_ones], writes=[psS0.b])
                P.add("pe", lambda e: e.matmul(psS1.t[:, :N], lhsT=onesb[:], rhs=sq16.t[:, :N], start=(c == 0), stop=(c == 7)),
                      reads=[sq16.b, b_ones], writes=[psS1.b])
                hp.release(cv16)
                hp.release(sq16)

            self.stats_mm = stats_mm

            def A2s(c):
                prod = tpool.alloc()
                pv = prod.t[:, 0:NS * KA].rearrange("p (s k) -> p s k", k=KA)
                P.add("dve", lambda e: e.tensor_tensor(
                    out=pv, in0=vbuf[:, c, 0:NS * KA].rearrange("p (k s) -> p s k", s=NS),
                    in1=hwdw[:, c, :].unsqueeze(1).to_broadcast([128, NS, KA]), op=ALU.mult),
                    reads=[b_vb[c], b_hwdw], writes=[prod.b])
                cvp = tp.alloc()
                P.add("dve", lambda e: e.tensor_reduce(out=cvp.t[:, :N], in_=pv, axis=mybir.AxisListType.X, op=ALU.add),
                      reads=[prod.b], writes=[cvp.b])
                tpool.release(prod)
                if self.pending is not None:
                    stats_mm(*self.pending)
                P.add("act", lambda e: e.activation(out=cv[:, c, :N], in_=cvp.t[:, :N], func=AF.Identity, bias=vcol(V_BDW, c)),
                      reads=[cvp.b, b_vecs], writes=[b_cv[c]])
                tp.release(cvp)
                cs16 = hp.alloc()
                P.add("dve", lambda e: e.tensor_tensor(out=cs16.t[:, N:2 * N], in0=cv[:, c, :N], in1=cv[:, c, :N], op=ALU.mult),
                      reads=[b_cv[c]], writes=[cs16.b])
                P.add("dve", lambda e: e.tensor_copy(out=cs16.t[:, :N], in_=cv[:, c, :N]), reads=[b_cv[c]], writes=[cs16.b])
                self.pending = (c, cs16, cs16)

            def A2(c):
                if is_sample:
                    return A2s(c)
                dA = diagA[c % 2]
                bdA = b_dA[c % 2]
                ps = psG.alloc()
                for k in range(KA):
                    P.add("pe", lambda e, k=k: e.matmul(
                        ps.t[:, :N], lhsT=dA[:, k, :], rhs=vbuf[:, c, k * stride:k * stride + N], start=(k == 0), stop=(k == KA - 1)),
                        reads=[bdA if k < KH else b_dAh[c % 2], b_vb[c]], writes=[ps.b])
                if c + 2 < 8:
                    if builds:
                        self.diag_build(c + 2)
                    else:
                        self.diag_load(c + 2)
                if self.pending is not None:
                    stats_mm(*self.pending)
                P.add("act", lambda e: e.activation(out=cv[:, c, :N], in_=ps.t[:, :N], func=AF.Identity, bias=vcol(V_BDW, c)),
                      reads=[ps.b, b_vecs], writes=[b_cv[c]])
                sq16 = hp.alloc()
                so = N if is_sample else 0
                P.add("dve", lambda e: e.scalar_tensor_tensor(
                    out=sq16.t[:, so:so + N], in0=ps.t[:, :N], scalar=vcol(V_BDW, c), in1=cv[:, c, :N], op0=ALU.add, op1=ALU.mult),
                    reads=[ps.b, b_vecs, b_cv[c]], writes=[sq16.b])
                psG.release(ps)
                cv16 = sq16 if is_sample else hp.alloc()
                P.add("dve", lambda e: e.tensor_copy(out=cv16.t[:, :N], in_=cv[:, c, :N]), reads=[b_cv[c]], writes=[cv16.b])
                self.pending = (c, cv16, sq16)
                if not is_sample and not is_last:
                    P.add("pool", lambda e: e.tensor_copy(out=vbuf[:, c, 0:30], in_=vbuf[:, c, TN:TN + 30]),
                          reads=[b_vb[c]], writes=[b_vb[c]])

            def S0(c):
                ps, o_bx = self.zget(24 + c)
                o0 = 3 * stride
                P.add("act", lambda e: e.activation(out=bxbuf[:, c, o0:o0 + N], in_=ps.t[:, :N], func=AF.Copy),
                      reads=[ps.b], writes=[b_bb[c]])
                if is_sample or is_last:
                    W = NS if is_sample else 3
                    P.add("act", lambda e: e.activation(out=stB[:, c, :W], in_=ps.t[:, N - W:N], func=AF.Copy),
                          reads=[ps.b], writes=[b_stB])
                self.zput(ps, o_bx)

            def build_dB(c):
                dB = diagB[c % 2]
                P.add("pool", lambda e: e.tensor_tensor(
                    out=dB[:], in0=identb[:].unsqueeze(1).to_broadcast([128, KB, 128]),
                    in1=wcb_sb[:, c, :].unsqueeze(2).to_broadcast([128, KB, 128]), op=ALU.mult),
                    reads=[b_idb, b_wcb], writes=[b_dB[c % 2]])

            build_dB(0)
            build_dB(1)

            def S1(c):
                dB = diagB[c % 2]
                bdB = b_dB[c % 2]
                ps = psG.alloc()
                for k in range(KB):
                    P.add("pe", lambda e, k=k: e.matmul(
                        ps.t[:, :N], lhsT=dB[:, k, :], rhs=bxbuf[:, c, k * stride:k * stride + N], start=(k == 0), stop=(k == KB - 1)),
                        reads=[bdB, b_bb[c]], writes=[ps.b])
                if c + 2 < 8:
                    build_dB(c + 2)
                if not is_sample and not is_last:
                    P.add("pool", lambda e: e.tensor_copy(out=bxbuf[:, c, 0:3], in_=bxbuf[:, c, TN:TN + 3]),
                          reads=[b_bb[c]], writes=[b_bb[c]])
                xb16 = hp.alloc()
                P.add("act", lambda e: e.activation(out=xb16.t[:, :N], in_=ps.t[:, :N], func=AF.Identity, bias=vcol(V_BCB, c)),
                      reads=[ps.b, b_vecs], writes=[xb16.b])
                xb = tp.alloc()
                P.add("act", lambda e: e.activation(out=xb.t[:, :N], in_=ps.t[:, :N], func=AF.Identity, bias=vcol(V_BCB, c)),
                      reads=[ps.b, b_vecs], writes=[xb.b])
                psG.release(ps)
                st[c]["xb"] = xb
                st[c]["xb16"] = xb16

            def S2a(c):
                xb16 = st[c]["xb16"]
                ps_r, ps_i = psG.alloc(), psG.alloc()
                P.add("pe", lambda e: e.matmul(ps_r.t[:, :N], lhsT=wr_sb[:, c * 128:(c + 1) * 128], rhs=xb16.t[:, :N],
                                               start=True, stop=True), reads=[b_wr, xb16.b], writes=[ps_r.b])
                P.add("pe", lambda e: e.matmul(ps_i.t[:, :N], lhsT=wi_sb[:, c * 128:(c + 1) * 128], rhs=xb16.t[:, :N],
                                               start=True, stop=True), reads=[b_wi, xb16.b], writes=[ps_i.b])
                hp.release(xb16)
                thr, thi = tp.alloc(), tp.alloc()
                P.add("act", lambda e: e.activation(out=thr.t[:, :N], in_=ps_r.t[:, :N], func=AF.Tanh, scale=0.5,
                                                    bias=dcol(D_HBR, c)), reads=[ps_r.b, b_der], writes=[thr.b])
                P.add("act", lambda e: e.activation(out=thi.t[:, :N], in_=ps_i.t[:, :N], func=AF.Tanh, scale=0.5,
                                                    bias=dcol(D_HBI, c)), reads=[ps_i.b, b_der], writes=[thi.b])
                psG.release(ps_r)
                psG.release(ps_i)
                a_, a2 = tp.alloc(), tp.alloc()
                P.add("act", lambda e: e.activation(out=a_.t[:, :N], in_=thr.t[:, :N], func=AF.Exp, scale=dcol(D_S1, c),
                                                    bias=dcol(D_S1, c)), reads=[thr.b, b_der], writes=[a_.b])
                P.add("act", lambda e: e.activation(out=a2.t[:, :N], in_=thr.t[:, :N], func=AF.Exp, scale=dcol(D_S2, c),
                                                    bias=dcol(D_S2, c)), reads=[thr.b, b_der], writes=[a2.b])
                tp.release(thr)
                st[c].update(thi=thi, a_=a_, a2=a2)

            def S2b(c):
                a2 = st[c]["a2"]
                P.add("act", lambda e: e.activation(out=a2.t[:, :N], in_=a2.t[:, :N], func=AF.Sqrt, scale=-1.0, bias=1.0),
                      reads=[a2.b], writes=[a2.b])

            def S3(c):
                ps_bg, o_bg = self.zget(32 + c)
                thg = tp.alloc()
                P.add("act", lambda e: e.activation(out=thg.t[:, :N], in_=ps_bg.t[:, :N], func=AF.Tanh, scale=0.5),
                      reads=[ps_bg.b], writes=[thg.b])
                P.add("dve", lambda e: e.scalar_tensor_tensor(
                    out=thg.t[:, :N], in0=thg.t[:, :N], scalar=1.0, in1=ps_bg.t[:, :N], op0=ALU.add, op1=ALU.mult),
                    reads=[thg.b, ps_bg.b], writes=[thg.b])
                self.zput(ps_bg, o_bg)
                st[c]["thg"] = thg

            def S4(c):
                xb, thi, a_, a2, thg = (st[c][k] for k in ("xb", "thi", "a_", "a2", "thg"))
                P.add("dve", lambda e: e.scalar_tensor_tensor(
                    out=thi.t[:, :N], in0=thi.t[:, :N], scalar=1.0, in1=xb.t[:, :N], op0=ALU.add, op1=ALU.mult),
                    reads=[thi.b, xb.b], writes=[thi.b])
                tp.release(xb)
                P.add("dve", lambda e: e.scalar_tensor_tensor(
                    out=thi.t[:, :N], in0=thi.t[:, :N], scalar=0.5, in1=a2.t[:, :N], op0=ALU.mult, op1=ALU.mult),
                    reads=[thi.b, a2.b], writes=[thi.b])
                tp.release(a2)
                hb_ = tp.alloc()
                if is_sample:
                    P.add("dve", lambda e: e.tensor_tensor(out=hb_.t[:, :N], in0=a_.t[:, :N], in1=shT[:, c, :], op=ALU.mult),
                          reads=[a_.b, b_shT], writes=[hb_.b])
                    P.add("dve", lambda e: e.tensor_tensor(out=hb_.t[:, :N], in0=hb_.t[:, :N], in1=thi.t[:, :N], op=ALU.add),
                          reads=[hb_.b, thi.b], writes=[hb_.b])
                    P.add("pool", lambda e: e.tensor_copy(out=stH[:, c, :NS], in_=hb_.t[:, :NS]),
                          reads=[hb_.b], writes=[b_stH])
                else:
                    P.add("dve", lambda e: e.tensor_tensor_scan(
                        out=hb_.t[:, :N], data0=a_.t[:, :N], data1=thi.t[:, :N], initial=hstate[:, c:c + 1],
                        op0=ALU.mult, op1=ALU.add), reads=[a_.b, thi.b, b_hsts[c]], writes=[hb_.b])
                    P.add("pool", lambda e: e.tensor_copy(out=hstate[:, c:c + 1], in_=hb_.t[:, N - 1:N]),
                          reads=[hb_.b], writes=[b_hsts[c]])
                    if is_last:
                        P.add("pool", lambda e: e.tensor_copy(out=stH[:, c, 0:1], in_=hb_.t[:, N - 1:N]),
                              reads=[hb_.b], writes=[b_stH])
                tp.release(a_)
                tp.release(thi)
                P.add("dve", lambda e: e.scalar_tensor_tensor(
                    out=self.mixT[:, 8 + c, :N], in0=hb_.t[:, :N], scalar=0.5, in1=thg.t[:, :N], op0=ALU.mult, op1=ALU.mult),
                    reads=[hb_.b, thg.b], writes=[self.b_mix[8 + c]])
                tp.release(hb_)
                tp.release(thg)

            fns = dict(A1=A1, A2=A2, S0=S0, S1=S1, S2a=S2a, S2b=S2b, S3=S3, S4=S4)
            def PRESQ(c):
                if self.nxt is not None:
                    self.nxt.pre_sq()

            def PREEL(c):
                if self.nxt is not None:
                    self.nxt.pre_elem()

            fns.update(PRESQ=PRESQ, PREEL=PREEL)
            for (name, c) in ab_schedule(is_sample):
                fns[name](c)
                yield name
                if name == "A1" and c == 7:
                    if is_sample:
                        state_out(stA, b_stA, NS, 0.5, na_s[:, 29, :])
                    elif is_last:
                        state_out(stA, b_stA, 30, 0.5, na_p[:, :])
                if name == "S0" and c == 7:
                    if is_sample:
                        state_out(stB, b_stB, NS, 1.0, nb_s[:, 2, :])
                    elif is_last:
                        state_out(stB, b_stB, 3, 1.0, nb_p[:, :])
            if is_sample:
                state_out(stH, b_stH, NS, 1.0, nh_s[:, :])
                for (b_, lst) in self.small:
                    for x in lst:
                        if x.b.last_w is not None:
                            b_.b.readers.append(x.b.last_w)
                        b_.b.readers.extend(x.b.readers)
                    tpool.release(b_)
            elif is_last:
                state_out(stH, b_stH, 1, 1.0, nh_p[:, :])

        def a3(self, nxt=None):
            N = self.N
            self.stats_mm(*self.pending)
            if self.is_sample:
                bh, lst = self.smallh
                for x in lst:
                    if x.b.last_w is not None:
                        bh.b.readers.append(x.b.last_w)
                    bh.b.readers.extend(x.b.readers)
                hpool.release(bh)
            m2 = tpool.alloc()
            pm = psS1 if self.is_sample else psS0
            mo = N if self.is_sample else 0
            P.add("act", lambda e: e.activation(out=mean_sb[:, :N], in_=pm.t[:, :N], func=AF.Copy), reads=[pm.b], writes=[b_mean])
            P.add("act", lambda e: e.activation(out=m2.t[:, :N], in_=pm.t[:, :N], func=AF.Square), reads=[pm.b], writes=[m2.b])
            P.add("dve", lambda e: e.tensor_tensor(out=m2.t[:, :N], in0=psS1.t[:, mo:mo + N], in1=m2.t[:, :N], op=ALU.subtract),
                  reads=[psS1.b, m2.b], writes=[m2.b])
            P.add("act", lambda e: e.activation(out=m2.t[:, :N], in_=m2.t[:, :N], func=AF.Sqrt, bias=EPS), reads=[m2.b], writes=[m2.b])
            P.add("dve", lambda e: e.reciprocal(out=rstd_ln[:, :N], in_=m2.t[:, :N]), reads=[m2.b], writes=[b_rln])
            tpool.release(m2)
            for c in range(8):
                P.add("pool", lambda e, c=c: e.tensor_tensor(out=cv[:, c, :N], in0=cv[:, c, :N], in1=mean_sb[:, :N], op=ALU.subtract),
                      reads=[b_cv[c], b_mean], writes=[b_cv[c]])
            for c in range(8):
                ps_g, o_g = self.zget(16 + c)
                t = tpool.alloc()
                P.add("pool", lambda e, t=t, c=c: e.tensor_tensor(out=t.t[:, :N], in0=cv[:, c, :N], in1=rstd_ln[:, :N], op=ALU.mult),
                      reads=[b_cv[c], b_rln], writes=[t.b])
                thg = tpool.alloc()
                P.add("act", lambda e, thg=thg, ps=ps_g: e.activation(out=thg.t[:, :N], in_=ps.t[:, :N], func=AF.Silu),
                      reads=[ps_g.b], writes=[thg.b])
                self.zput(ps_g, o_g)
                thl = tpool.alloc()
                P.add("act", lambda e, t=t, thl=thl, c=c: e.activation(out=thl.t[:, :N], in_=t.t[:, :N], func=AF.Silu,
                                                                       scale=vcol(V_LNG, c), bias=vcol(V_LNB, c)),
                      reads=[t.b, b_vecs], writes=[thl.b])
                tpool.release(t)
                P.add("dve", lambda e, thl=thl, thg=thg, c=c: e.tensor_tensor(
                    out=self.mixT[:, c, :N], in0=thl.t[:, :N], in1=thg.t[:, :N], op=ALU.mult),
                    reads=[thl.b, thg.b], writes=[self.b_mix[c]])
                tpool.release(thl)
                tpool.release(thg)

        def post_gen(self, hold=False, pieces=None):
            subs, xd, yd = self.subs, self.xd, self.yd
            pT, b_pT = self.pT, self.b_pT
            mixT_, b_mix_ = self.mixT, self.b_mix
            wide = self.is_sample
            if wide:
                P.add("pool", lambda e: e.tensor_copy(out=mixT[:, :, 0:NS], in_=mixT_s[:, :, :]),
                      reads=list(b_mix_s), writes=list(b_mix))
                mixT_, b_mix_ = mixT, b_mix
            if pieces is None:
                wout = [use_piece(40 + k) for k in range(16)]
                wpg = [use_piece(56 + k) for k in range(8)]
                wpe = [use_piece(64 + k) for k in range(2)]
            else:
                wout, wpg, wpe = pieces
            self.post_pieces = (wout, wpg, wpe)
            nsub = len(subs)
            sst = [dict() for _ in range(nsub)]

            def M(s):
                r0, R = subs[s]
                if getattr(self, "xr_pre", None) is not None:
                    xr = self.xr_pre
                else:
                    xr = rowpool.alloc()
                    P.add("sp", lambda e: e.dma_start(out=xr.t[:R, :], in_=xd[r0:r0 + R, :]), writes=[xr.b], dma_key=xr.b)
                psm = [psG.alloc(), psG.alloc()]
                for hf in range(2):
                    for kc in range(16):
                        P.add("pe", lambda e, ps=psm[hf], kc=kc, hf=hf: e.matmul(
                            ps.t[:(128 if wide else R), :], lhsT=mixT_[:, kc, s * 128:s * 128 + (128 if wide else R)],
                            rhs=wout[kc].t[:, hf * 512:(hf + 1) * 512], start=(kc == 0), stop=(kc == 15)),
                            reads=[b_mix_[kc], wout[kc].b], writes=[psm[hf].b])
                if s == nsub - 1 and not hold:
                    release(wout)
                for hf in range(2):
                    P.add("dve", lambda e, ps=psm[hf], hf=hf: e.tensor_tensor(
                        out=xr.t[:R, hf * 512:(hf + 1) * 512], in0=ps.t[:R, :], in1=xr.t[:R, hf * 512:(hf + 1) * 512], op=ALU.add),
                        reads=[psm[hf].b, xr.b], writes=[xr.b])
                    psG.release(psm[hf])
                h16 = tpool.alloc()
                P.add("act", lambda e: e.activation(out=h16.bf()[:R, :], in_=xr.t[:R, :], func=AF.Copy),
                      reads=[xr.b], writes=[h16.b])
                sst[s].update(xr=xr, h16=h16)

            def T1(s):
                r0, R = subs[s]
                xr, h16 = sst[s]["xr"], sst[s]["h16"]
                for kc in range(8):
                    P.add("pe", lambda e, kc=kc: e.transpose(
                        out=psS0.bf()[:, kc * 128:kc * 128 + R], in_=h16.bf()[:R, kc * 128:(kc + 1) * 128], identity=identb[:R, :R]),
                        reads=[h16.b, b_idb], writes=[psS0.b])
                tpool.release(h16)
                h1T = tpool.alloc()
                P.add("act", lambda e: e.activation(
                    out=h1T.bf().rearrange("p (a b) -> p a b", b=128)[:, :, :R],
                    in_=psS0.bf().rearrange("p (a b) -> p a b", b=128)[:, :, :R], func=AF.Copy),
                    reads=[psS0.b], writes=[h1T.b])
                sst[s]["h1T"] = h1T

            def T2(s):
                r0, R = subs[s]
                xr, h1T = sst[s]["xr"], sst[s]["h1T"]
                pspg = [psG.alloc(), psG.alloc()]
                for hf in range(2):
                    for kc in range(8):
                        P.add("pe", lambda e, ps=pspg[hf], kc=kc, hf=hf: e.matmul(
                            ps.t[:(128 if wide else R), :], lhsT=h1T.bf()[:, kc * 128:kc * 128 + (128 if wide else R)],
                            rhs=wpg[kc].t[:, hf * 512:(hf + 1) * 512],
                            start=(kc == 0), stop=(kc == 7)),
                            reads=[h1T.b, wpg[kc].b], writes=[pspg[hf].b])
                tpool.release(h1T)
                pspe = [psG.alloc(), psG.alloc()]
                for hf in range(2):
                    for j in range(2):
                        P.add("pe", lambda e, ps=pspe[hf], j=j, hf=hf: e.matmul(
                            ps.t[:R, :], lhsT=pT[:, j, s * 128:s * 128 + R], rhs=wpe[j].t[:, hf * 512:(hf + 1) * 512],
                            start=(j == 0), stop=(j == 1)),
                            reads=[b_pT, wpe[j].b], writes=[pspe[hf].b])
                if s == nsub - 1 and not hold:
                    release(wpg + wpe)
                for hf in range(2):
                    thp = tpool.alloc()
                    P.add("act", lambda e, thp=thp, ps=pspg[hf]: e.activation(out=thp.t[:R, :], in_=ps.t[:R, :], func=AF.Tanh, scale=0.5),
                          reads=[pspg[hf].b], writes=[thp.b])
                    P.add("dve", lambda e, thp=thp, ps=pspe[hf]: e.scalar_tensor_tensor(
                        out=thp.t[:R, :], in0=thp.t[:R, :], scalar=1.0, in1=ps.t[:R, :], op0=ALU.add, op1=ALU.mult),
                        reads=[thp.b, pspe[hf].b], writes=[thp.b])
                    P.add("dve", lambda e, thp=thp, hf=hf: e.scalar_tensor_tensor(
                        out=xr.t[:R, hf * 512:(hf + 1) * 512], in0=thp.t[:R, :], scalar=0.5, in1=xr.t[:R, hf * 512:(hf + 1) * 512],
                        op0=ALU.mult, op1=ALU.add),
                        reads=[thp.b, xr.b], writes=[xr.b])
                    tpool.release(thp)
                    psG.release(pspg[hf])
                    psG.release(pspe[hf])

            def SQ(s):
                r0, R = subs[s]
                xr = sst[s]["xr"]
                junk = tpool.alloc()
                sq2 = smalls[:, 16 + s:17 + s]
                P.add("act", lambda e: e.activation(out=junk.bf()[:R, :], in_=xr.t[:R, :], func=AF.Square, accum_out=sq2[:R, :]),
                      reads=[xr.b], writes=[junk.b, b_sq2s[0]])
                tpool.release(junk)

            def FIN():
                R0 = subs[0][1]
                sqa = smalls[:, 16:16 + nsub]
                P.add("act", lambda e: e.activation(out=sqa[:R0, :], in_=sqa[:R0, :], func=AF.Sqrt, scale=1.0 / D, bias=EPS),
                      reads=[b_sq2s[0]], writes=[b_sq2s[0]])
                P.add("dve", lambda e: e.reciprocal(out=sqa[:R0, :], in_=sqa[:R0, :]), reads=[b_sq2s[0]], writes=[b_sq2s[0]])
                for s in range(nsub):
                    r0, R = subs[s]
                    xr = sst[s]["xr"]
                    P.add("dve", lambda e, xr=xr, R=R, s=s: e.scalar_tensor_tensor(
                        out=xr.t[:R, :], in0=xr.t[:R, :], scalar=smalls[:R, 16 + s:17 + s], in1=gfb_sb[:R, :],
                        op0=ALU.mult, op1=ALU.mult),
                        reads=[xr.b, b_sq2s[0], b_gfb], writes=[xr.b])
                    P.add("sp", lambda e, xr=xr, R=R, r0=r0: e.dma_start(out=yd[r0:r0 + R, :], in_=xr.t[:R, :]),
                          reads=[xr.b], dma_key=xr.b)
                    rowpool.release(xr)

            order = []
            for s in range(nsub + 2):
                if s < nsub:
                    order.append((M, s))
                if 0 <= s - 2 < nsub:
                    order.append((T2, s - 2))
                if 0 <= s - 1 < nsub:
                    order.append((T1, s - 1))
                if 0 <= s - 2 < nsub:
                    order.append((SQ, s - 2))
            for (fn, s) in order:
                fn(s)
                yield fn.__name__ + str(s)
            FIN()
            yield "FIN"

    tiles = []
    for t in range(SEQ // TN):
        subs = [(t * TN + i * 128, 128) for i in range(TN // 128)]
        tiles.append(Tile(t, TN, subs, x_p, p_p, y_p, False, t == 0, t == SEQ // TN - 1))
    tiles.append(Tile(SEQ // TN, NS, [(0, NS)], x_s, p_s, y_s, True, False, False))
    tiles[0].pre()
    sample = tiles[-1]
    sample.nxt = None
    prompts = tiles[:-1]
    for i, T_ in enumerate(prompts):
        nxt = prompts[i + 1] if i + 1 < len(prompts) else None
        last = (nxt is None)
        T_.nxt = nxt
        if last:
            T_.comp = sample
        for _ in T_.ab_gen():
            pass
        if last:
            sample.sample_state()
        T_.a3(nxt)
        if last:
            sample.sample_state_trans()
        if nxt is not None:
            nxt.pre_trans()
        if i == len(prompts) - 2:
            sample.pre()
        if last:
            for _ in sample.ab_gen():
                pass
            sample.a3(None)
            xr0 = rowpool.alloc()
            P.add("sp", lambda e: e.dma_start(out=xr0.t[:NS, :], in_=x_s[0:NS, :]), writes=[xr0.b], dma_key=xr0.b)
            sample.xr_pre = xr0
            for _ in T_.post_gen(hold=True):
                pass
            for _ in sample.post_gen(pieces=T_.post_pieces):
                pass
        else:
            for _ in T_.post_gen():
                pass
    P.emit()
    return nc


_NC_CACHE = {}


def kernel(x_prompt, x_sample, p_prompt, p_sample, state_conv_a, state_conv_b, state_h,
           g_norm, w_in, w_dw_a, b_dw_a, ln_g, ln_b, w_conv_b, b_conv_b,
           w_r, b_r, w_i, b_i, lam, w_out, w_pe, w_pg, g_final):
    f = lambda a: np.ascontiguousarray(np.asarray(a, dtype=np.float32))
    x_prompt, x_sample, p_prompt, p_sample = f(x_prompt), f(x_sample), f(p_prompt), f(p_sample)
    state_conv_a, state_conv_b, state_h = f(state_conv_a), f(state_conv_b), f(state_h)
    w_in_l = f(f(w_in)[0].reshape(8, 128, 40, 128).transpose(2, 1, 0, 3).reshape(40, 128, 1024))
    w_out_l = f(f(w_out)[0].reshape(16, 128, 1024))
    w_pg_l = f(f(w_pg)[0].reshape(8, 128, 1024))
    w_pe_l = f(f(w_pe)[0].reshape(2, 128, 1024))
    w_r_l = f(f(w_r)[0].transpose(1, 0, 2).reshape(128, 1024))
    w_i_l = f(f(w_i)[0].transpose(1, 0, 2).reshape(128, 1024))
    fm = lambda v: f(v).reshape(8, 128).T
    vecs = f(np.stack([fm(g_norm), fm(b_dw_a), fm(ln_g), fm(ln_b), fm(b_conv_b), fm(b_r), fm(b_i), fm(lam)], axis=1)
             .reshape(128, 64))
    wdw = f(f(w_dw_a)[0].reshape(KA, 8, 128).transpose(2, 1, 0).reshape(128, 8 * KA))
    wcb = f(f(w_conv_b)[0].reshape(KB, 8, 128).transpose(2, 1, 0).reshape(128, 8 * KB))
    gfb = f(np.broadcast_to(f(g_final).reshape(1, D), (128, D)))
    ident = np.eye(128, dtype=np.float32)

    if "nc" not in _NC_CACHE:
        _NC_CACHE["nc"] = build_nc()
    nc = _NC_CACHE["nc"]
    in_maps = []
    for i in range(NCORES):
        sl = slice(i * NS, (i + 1) * NS)
        in_maps.append(dict(
            x_p=x_prompt[i], p_p=p_prompt[0, i], x_s=f(x_sample[sl, 0]), p_s=f(p_sample[0, sl, 0]),
            sca=f(state_conv_a[0, sl].reshape(NS * 30, D)), scb=f(state_conv_b[0, sl].reshape(NS * 3, D)),
            sh=f(state_h[0, sl]),
            w_in=w_in_l, w_out=w_out_l, w_pg=w_pg_l, w_pe=w_pe_l, w_r=w_r_l, w_i=w_i_l,
            vecs=vecs, wdw=wdw, wcb=wcb, gfb=gfb, ident=ident))
    res = run_bass_kernel_spmd(nc, in_maps, core_ids=list(range(NCORES)))
    r = res.results
    y_prompt = np.stack([r[i]["y_p"] for i in range(NCORES)], 0)
    y_sample = np.concatenate([r[i]["y_s"] for i in range(NCORES)], 0)[:, None, :]
    na_p = np.stack([r[i]["na_p"] for i in range(NCORES)], 0)[None]
    nb_p = np.stack([r[i]["nb_p"] for i in range(NCORES)], 0)[None]
    nh_p = np.concatenate([r[i]["nh_p"] for i in range(NCORES)], 0)[None]
    na_s = np.concatenate([r[i]["na_s"] for i in range(NCORES)], 0)[None]
    nb_s = np.concatenate([r[i]["nb_s"] for i in range(NCORES)], 0)[None]
    nh_s = np.concatenate([r[i]["nh_s"] for i in range(NCORES)], 0)[None]
    out = (y_prompt, y_sample, na_p, nb_p, nh_p, na_s, nb_s, nh_s)
    return tuple(np.ascontiguousarray(o.astype(np.float32)) for o in out)
```
